# Optimizing a Trainium2 kernel written in Bass

```python
import math
import jax, jax.numpy as jnp
from jax import lax
import numpy as np

D_MODEL = 1024
BATCH = 16
SEQ = 4096
DEPTH = 2
DEC_BATCH = 32
DEC_SEQ = 16
PAST_LEN = 1024

CHUNK = 64
N_MIXERS = 2
N_A = (DEPTH + 1) // 2
N_B = DEPTH // 2
EPS = 1e-5

EXPAND = 2
D_INNER = EXPAND * D_MODEL
SSM_HEAD_DIM = 64
SSM_HEADS = D_INNER // SSM_HEAD_DIM
SSM_STATE = 128
SSM_GROUPS = 8
HEADS_PER_GROUP = SSM_HEADS // SSM_GROUPS
GN = SSM_GROUPS * SSM_STATE
CONV_WIDTH = 4
CONV_DIM = D_INNER + 2 * GN
SSD_CHUNK = CHUNK
A_PROJ = D_INNER + CONV_DIM + SSM_HEADS
NORM_GROUP = D_INNER // SSM_GROUPS

G_WIDTH = EXPAND * D_MODEL
G_CHUNK = 128
G_GROUPS = 8
G_GROUP_DIM = G_WIDTH // G_GROUPS
B_PROJ = 3 * G_WIDTH

kernel_name = 'hybrid_ssd_gmlp_streaming_step'


def rms_norm(x, w):
    xf = x.astype(jnp.float32)
    y = xf * lax.rsqrt(jnp.mean(xf * xf, axis=-1, keepdims=True) + EPS)
    return (y * w.astype(jnp.float32)).astype(x.dtype)


def layer_norm(x, w, b):
    xf = x.astype(jnp.float32)
    xc = xf - jnp.mean(xf, axis=-1, keepdims=True)
    y = xc * lax.rsqrt(jnp.mean(xc * xc, axis=-1, keepdims=True) + EPS)
    return (y * w.astype(jnp.float32) + b.astype(jnp.float32)).astype(x.dtype)


def gated_group_rms_norm(y, z, w):
    g = y.astype(jnp.float32) * jax.nn.silu(z.astype(jnp.float32))
    shp = g.shape
    g = g.reshape(shp[:-1] + (SSM_GROUPS, NORM_GROUP))
    g = g * lax.rsqrt(jnp.mean(g * g, axis=-1, keepdims=True) + EPS)
    return g.reshape(shp) * w.astype(jnp.float32)


def causal_dwconv(xbc, conv_prev, w, b):
    seq_len = xbc.shape[1]
    xp = jnp.concatenate([conv_prev.astype(xbc.dtype), xbc], axis=1)
    out = b + w[0] * xp[:, 0:seq_len]
    for k in range(1, CONV_WIDTH):
        out = out + w[k] * xp[:, k:k + seq_len]
    return out, xp[:, seq_len:]


def ssd_scan(x, dt, a, bm, cm, h0):
    f32 = jnp.float32
    bsz, seq_len = x.shape[:2]
    q = min(SSD_CHUNK, seq_len)
    assert seq_len % q == 0
    nc = seq_len // q

    def to_chunks(t):
        t = t.astype(f32).reshape((bsz, nc, q) + t.shape[2:])
        return jnp.moveaxis(t, 1, 0)

    xs = to_chunks(x.reshape(bsz, seq_len, SSM_GROUPS, HEADS_PER_GROUP, SSM_HEAD_DIM))
    dts = to_chunks(dt.reshape(bsz, seq_len, SSM_GROUPS, HEADS_PER_GROUP))
    bs = to_chunks(bm)
    cs = to_chunks(cm)
    a_g = a.astype(f32).reshape(SSM_GROUPS, HEADS_PER_GROUP)
    causal = jnp.tril(jnp.ones((q, q), dtype=bool))[None, :, :, None, None]

    def step(h, inp):
        xc, dtc, bc, cc = inp
        la = jnp.cumsum(dtc * a_g, axis=1)
        seg = la[:, :, None] - la[:, None, :]
        decay = jnp.exp(jnp.where(causal, seg, -jnp.inf))
        cb = jnp.einsum('btgn,bsgn->btsg', cc, bc)
        y_in = jnp.einsum('btsg,btsgr,bsgr,bsgrp->btgrp', cb, decay, dtc, xc)
        y_st = jnp.einsum('btgn,bgrpn->btgrp', cc, h) * jnp.exp(la)[..., None]
        last = la[:, -1]
        w_in = jnp.exp(last[:, None] - la) * dtc
        h_new = h * jnp.exp(last)[..., None, None] + jnp.einsum('bsgn,bsgr,bsgrp->bgrpn', bc, w_in, xc)
        return h_new, y_in + y_st

    h0g = h0.astype(f32).reshape(bsz, SSM_GROUPS, HEADS_PER_GROUP, SSM_HEAD_DIM, SSM_STATE)
    h_last, ys = lax.scan(step, h0g, (xs, dts, bs, cs))
    y = jnp.moveaxis(ys, 0, 1).reshape(bsz, seq_len, SSM_HEADS, SSM_HEAD_DIM)
    return y, h_last.reshape(bsz, SSM_HEADS, SSM_HEAD_DIM, SSM_STATE)


def mamba2_mixer(h, conv_prev, ssm_prev, w_in, conv_w, conv_b, dt_bias, a_log, d_skip, norm_w, w_out):
    bsz, seq_len = h.shape[:2]
    proj = h @ w_in
    z = proj[..., :D_INNER]
    xbc = proj[..., D_INNER:D_INNER + CONV_DIM]
    dt_raw = proj[..., D_INNER + CONV_DIM:]
    xbc_c, conv_new = causal_dwconv(xbc, conv_prev, conv_w, conv_b)
    xbc_c = jax.nn.silu(xbc_c)
    xs = xbc_c[..., :D_INNER].reshape(bsz, seq_len, SSM_HEADS, SSM_HEAD_DIM)
    bm = xbc_c[..., D_INNER:D_INNER + GN].reshape(bsz, seq_len, SSM_GROUPS, SSM_STATE)
    cm = xbc_c[..., D_INNER + GN:].reshape(bsz, seq_len, SSM_GROUPS, SSM_STATE)
    dt = jax.nn.softplus(dt_raw.astype(jnp.float32) + dt_bias.astype(jnp.float32))
    a = -jnp.exp(a_log.astype(jnp.float32))
    y, ssm_new = ssd_scan(xs, dt, a, bm, cm, ssm_prev)
    y = y + d_skip.astype(jnp.float32)[:, None] * xs.astype(jnp.float32)
    y = gated_group_rms_norm(y.reshape(bsz, seq_len, D_INNER), z, norm_w).astype(h.dtype)
    return y @ w_out, conv_new, ssm_new.astype(ssm_prev.dtype)


def gmlp_mixer(h, w_in, ln_w, ln_b, w_sp, b_sp, w_out):
    bsz, seq_len = h.shape[:2]
    proj = h @ w_in
    u = jax.nn.gelu(proj[..., :G_WIDTH], approximate=False)
    v = jax.nn.gelu(proj[..., G_WIDTH:2 * G_WIDTH], approximate=False)
    z = proj[..., 2 * G_WIDTH:]
    v = layer_norm(v, ln_w, ln_b)
    q = min(G_CHUNK, seq_len)
    assert seq_len % q == 0
    nk = seq_len // q
    pos = jnp.arange(q)
    mask = (pos[None, :] // CHUNK) <= (pos[:, None] // CHUNK)
    wm = jnp.where(mask[None], w_sp[:, :q, :q], 0)
    vc = v.reshape(bsz, nk, q, G_GROUPS, G_GROUP_DIM)
    s = jnp.einsum('gts,bksgc->bktgc', wm, vc) + b_sp[:, :q].T[:, :, None]
    y = (u * s.reshape(bsz, seq_len, G_WIDTH) * jax.nn.silu(z)).astype(h.dtype)
    return y @ w_out, v


def setup_inputs(seed: int = 0) -> dict:
    key = jax.random.key(seed)
    ks = jax.random.split(key, 20)
    f32 = jnp.float32

    def nrm(k, shape, scale):
        return scale * jax.random.normal(k, shape, f32)

    x_prompt = nrm(ks[0], (BATCH, SEQ, D_MODEL), 1.0)
    x_sample = nrm(ks[1], (DEC_BATCH, DEC_SEQ, D_MODEL), 1.0)
    cache_conv = nrm(ks[2], (N_A, DEC_BATCH, CONV_WIDTH - 1, CONV_DIM), 1.0)
    state_ssm = nrm(ks[3], (N_A, DEC_BATCH, SSM_HEADS, SSM_HEAD_DIM, SSM_STATE), 0.1)
    norm_w = 1.0 + nrm(ks[4], (DEPTH, D_MODEL), 0.1)
    final_norm_w = 1.0 + nrm(ks[5], (D_MODEL,), 0.1)
    a_w_in = nrm(ks[6], (N_A, D_MODEL, A_PROJ), D_MODEL ** -0.5)
    a_conv_w = nrm(ks[7], (N_A, CONV_WIDTH, CONV_DIM), CONV_WIDTH ** -0.5)
    a_conv_b = nrm(ks[8], (N_A, CONV_DIM), 0.02)
    dt0 = jnp.exp(jax.random.uniform(ks[9], (N_A, SSM_HEADS), f32, math.log(1e-3), math.log(1e-1)))
    a_dt_bias = dt0 + jnp.log(-jnp.expm1(-dt0))
    a_log = jnp.log(jax.random.uniform(ks[10], (N_A, SSM_HEADS), f32, 1.0, 16.0))
    a_d = 1.0 + nrm(ks[11], (N_A, SSM_HEADS), 0.1)
    a_norm_w = 1.0 + nrm(ks[12], (N_A, D_INNER), 0.1)
    a_w_out = nrm(ks[13], (N_A, D_INNER, D_MODEL), D_INNER ** -0.5)
    b_w_in = nrm(ks[14], (N_B, D_MODEL, B_PROJ), D_MODEL ** -0.5)
    b_ln_w = 1.0 + nrm(ks[15], (N_B, G_WIDTH), 0.1)
    b_ln_b = nrm(ks[16], (N_B, G_WIDTH), 0.02)
    b_w_sp = nrm(ks[17], (N_B, G_GROUPS, G_CHUNK, G_CHUNK), G_CHUNK ** -0.5)
    b_b_sp = 1.0 + nrm(ks[18], (N_B, G_GROUPS, G_CHUNK), 0.1)
    b_w_out = nrm(ks[19], (N_B, G_WIDTH, D_MODEL), G_WIDTH ** -0.5)
    return {'x_prompt': x_prompt, 'x_sample': x_sample,
            'cache_conv': cache_conv, 'state_ssm': state_ssm,
            'norm_w': norm_w, 'final_norm_w': final_norm_w,
            'a_w_in': a_w_in, 'a_conv_w': a_conv_w, 'a_conv_b': a_conv_b,
            'a_dt_bias': a_dt_bias, 'a_log': a_log, 'a_d': a_d,
            'a_norm_w': a_norm_w, 'a_w_out': a_w_out,
            'b_w_in': b_w_in, 'b_ln_w': b_ln_w, 'b_ln_b': b_ln_b,
            'b_w_sp': b_w_sp, 'b_b_sp': b_b_sp, 'b_w_out': b_w_out}


def reference(x_prompt, x_sample, cache_conv, state_ssm, norm_w, final_norm_w,
              a_w_in, a_conv_w, a_conv_b, a_dt_bias, a_log, a_d, a_norm_w, a_w_out,
              b_w_in, b_ln_w, b_ln_b, b_w_sp, b_b_sp, b_w_out):
    assert PAST_LEN % G_CHUNK == 0
    hp, hs = x_prompt, x_sample
    bp = x_prompt.shape[0]
    conv_p, ssm_p, conv_s, ssm_s, v_s = [], [], [], [], []
    for i in range(DEPTH):
        j = i // N_MIXERS
        hn_p = rms_norm(hp, norm_w[i])
        hn_s = rms_norm(hs, norm_w[i])
        if i % N_MIXERS == 0:
            a_par = (a_w_in[j], a_conv_w[j], a_conv_b[j], a_dt_bias[j], a_log[j],
                     a_d[j], a_norm_w[j], a_w_out[j])
            conv0 = jnp.zeros((bp, CONV_WIDTH - 1, CONV_DIM), hp.dtype)
            ssm0 = jnp.zeros((bp, SSM_HEADS, SSM_HEAD_DIM, SSM_STATE), jnp.float32)
            out_p, c_p, s_p = mamba2_mixer(hn_p, conv0, ssm0, *a_par)
            out_s, c_s, s_s = mamba2_mixer(hn_s, cache_conv[j], state_ssm[j], *a_par)
            conv_p.append(c_p)
            ssm_p.append(s_p)
            conv_s.append(c_s)
            ssm_s.append(s_s)
        else:
            b_par = (b_w_in[j], b_ln_w[j], b_ln_b[j], b_w_sp[j], b_b_sp[j], b_w_out[j])
            out_p, _ = gmlp_mixer(hn_p, *b_par)
            out_s, v_new = gmlp_mixer(hn_s, *b_par)
            v_s.append(v_new)
        hp = hp + out_p
        hs = hs + out_s
    y_prompt = rms_norm(hp, final_norm_w)
    y_sample = rms_norm(hs, final_norm_w)
    return (y_prompt, y_sample, jnp.stack(conv_p), jnp.stack(ssm_p),
            jnp.stack(conv_s), jnp.stack(ssm_s), jnp.stack(v_s))
```

```python
import numpy as np
from contextlib import ExitStack
import concourse.bass as bass
import concourse.mybir as mybir
from concourse.bass_utils import run_bass_kernel_spmd

F32 = mybir.dt.float32
BF16 = mybir.dt.bfloat16
AF = mybir.ActivationFunctionType
ALU = mybir.AluOpType

EPS = 1e-5
D = 1024
DI = 2048
NH = 32
HD = 64
NS = 128
A_PROJ = 6176
B_PROJ = 6144
LS = 16
NSEQ_P = 2
NSEQ_S = 4
ENGS = ('pe', 'act', 'dve', 'pool', 'sp')


class _Probe:
    def __init__(self):
        self.kind, self.n, self.func = None, 0, None

    def _fs(self, ap):
        n = 1
        for d in list(ap.shape)[1:]:
            n *= int(d)
        return n

    def __getattr__(self, name):
        def f(*a, **k):
            self.kind = name
            out = k.get('out', a[0] if a else None)
            if name == 'matmul':
                self.n = self._fs(k.get('rhs', a[2] if len(a) > 2 else out))
            elif name == 'dma_start':
                self.n = self._fs(out) * int(out.shape[0])
            else:
                self.n = self._fs(out)
                two = ('in1' in k) or name == 'scalar_tensor_tensor'
                if two:
                    self.n *= 2
            self.func = k.get('func')
            return self
        return f

    def then_inc(self, *a, **k):
        return self


_TABLE = {}


def _table_of(func):
    if func is None:
        return None
    nm = str(func).split('.')[-1]
    return {'Silu': 'silu', 'Gelu': 'gelu', 'Exp': 'exp', 'Ln': 'exp'}.get(nm)


class Prog:
    LAT = 1.2

    def __init__(self, nc, es):
        self.nc, self.es = nc, es
        self.nodes = []
        self.lastw, self.readers = {}, {}
        self.pend_pe = None
        self.last_dma = {}
        self.sem = {}
        self.bar = None

    def _mksem(self, name):
        self.sem[name] = self.es.enter_context(self.nc.semaphore("s_" + name))

    def barrier(self, fn):
        self._close_pe()
        has_succ = set()
        for nd in self.nodes:
            has_succ |= set(nd['deps'])
        sinks = set(i for i in range(len(self.nodes)) if i not in has_succ)
        nid = len(self.nodes)
        self.nodes.append(dict(eng='pool', fns=[fn], deps=sinks, dur=0.2, table=None, key=None, lat=0.0))
        self.bar = nid

    def _edges(self, reads, writes):
        d = set()
        if self.bar is not None:
            d.add(self.bar)
        for b in reads:
            if b in self.lastw:
                d.add(self.lastw[b])
        for b in writes:
            if b in self.lastw:
                d.add(self.lastw[b])
            d.update(self.readers.get(b, ()))
        return d

    def _touch(self, nid, reads, writes):
        for b in reads:
            self.readers.setdefault(b, set()).add(nid)
        for b in writes:
            self.lastw[b] = nid
            self.readers[b] = set()

    def _cost(self, eng, fn):
        p = _Probe()
        fn(p)
        n = p.n
        if eng == 'pe':
            return max(0.06, n / 2400.0) + (0.05 if n <= 128 else 0.0), None
        if p.kind == 'dma_start':
            return 0.1, None
        if eng == 'act':
            return 0.22 + n / 1400.0, _table_of(p.func)
        if eng == 'dve':
            return 0.07 + n / 960.0, None
        return 0.15 + n / 420.0, None

    def op(self, eng, fn, reads=(), writes=(), inc=True):
        dur, table = self._cost(eng, fn)
        if eng == 'pe':
            if self.pend_pe is None:
                self.pend_pe = dict(eng='pe', fns=[], deps=set(), dur=0.0, table=None, key=None, lat=0.0, reads=[], writes=[])
            g = self.pend_pe
            g['fns'].append(fn)
            g['dur'] += dur
            g['deps'] |= self._edges(reads, writes)
            g['reads'] += list(reads)
            g['writes'] += list(writes)
            if inc:
                self._close_pe()
            return
        nid = len(self.nodes)
        self.nodes.append(dict(eng=eng, fns=[fn], deps=self._edges(reads, writes), dur=dur, table=table, key=None, lat=0.0))
        self._touch(nid, reads, writes)

    def _close_pe(self):
        g = self.pend_pe
        if g is None:
            return
        self.pend_pe = None
        nid = len(self.nodes)
        g['deps'].discard(nid)
        self.nodes.append(g)
        self._touch(nid, g.pop('reads'), g.pop('writes'))

    def dma(self, eng, key, fn, reads=(), writes=()):
        if key not in self.sem:
            self._mksem(key)
        p = _Probe()
        fn(p)
        nid = len(self.nodes)
        deps = self._edges(reads, writes)
        if key in self.last_dma:
            deps.add(self.last_dma[key])
        self.last_dma[key] = nid
        self.nodes.append(dict(eng=eng, fns=[fn], deps=deps, dur=0.15 if eng == 'sp' else 1.0, table=None, key=key,
                               lat=2.0 + p.n * 4 / 150e3))
        self._touch(nid, reads, writes)

    def schedule(self):
        import heapq
        self._close_pe()
        N = len(self.nodes)
        succ = [[] for _ in range(N)]
        ndep = [0] * N
        for i, nd in enumerate(self.nodes):
            nd['deps'] = [d for d in nd['deps'] if d != i]
            ndep[i] = len(nd['deps'])
            for d in nd['deps']:
                succ[d].append(i)
        rank = [0.0] * N
        for i in range(N - 1, -1, -1):
            nd = self.nodes[i]
            r = 0.0
            for s_ in succ[i]:
                sn = self.nodes[s_]
                lat = 0.0 if (sn['eng'] == nd['eng'] and nd['key'] is None) else self.LAT
                r = max(r, rank[s_] + lat)
            rank[i] = r + nd['dur'] + nd['lat']
        fin = [0.0] * N
        heaps = {e: [] for e in ENGS}
        free = {e: 0.0 for e in ENGS}
        cur_table = [None]
        order = {e: [] for e in ENGS}

        def ready_time(i):
            nd = self.nodes[i]
            t = 0.0
            for d in nd['deps']:
                dn = self.nodes[d]
                lat = 0.0 if (dn['eng'] == nd['eng'] and dn['key'] is None) else self.LAT
                t = max(t, fin[d] + lat)
            return t

        for i in range(N):
            if ndep[i] == 0:
                heapq.heappush(heaps[self.nodes[i]['eng']], (0.0, i))
        done = 0
        while done < N:
            best = None
            for e in ENGS:
                h = heaps[e]
                if not h:
                    continue
                cands = heapq.nsmallest(6, h) if len(h) > 1 else [h[0]]
                loc = None
                st0 = max(free[e], cands[0][0])
                for (rt, i) in cands:
                    st = max(free[e], rt)
                    pen = 0.0
                    tb = self.nodes[i]['table']
                    if e == 'act' and tb is not None and tb != cur_table[0]:
                        pen = 1.3
                    if st + 2.0 * pen > st0 + 1.5 and loc is not None:
                        continue
                    k2 = (-rank[i], i)
                    if loc is None or k2 < loc[0]:
                        loc = (k2, st, pen, rt, i)
                (_, st, pen, rt, i) = loc
                key = (st + pen, i)
                if best is None or key < best[0]:
                    best = (key, e, rt, i, pen)
            (_, e, rt, i, pen) = best
            h = heaps[e]
            if h[0][1] == i:
                heapq.heappop(h)
            else:
                h.remove((rt, i))
                heapq.heapify(h)
            nd = self.nodes[i]
            st = max(free[e], rt) + pen
            free[e] = st + nd['dur']
            fin[i] = free[e] + nd['lat']
            if e == 'act' and nd['table'] is not None:
                cur_table[0] = nd['table']
            order[e].append(i)
            done += 1
            for s_ in succ[i]:
                ndep[s_] -= 1
                if ndep[s_] == 0:
                    heapq.heappush(heaps[self.nodes[s_]['eng']], (ready_time(s_), s_))
        self.order = order
        self.sim_time = max(free.values())
        return order

    def emit(self, final_keys=None):
        nc = self.nc
        order = self.schedule()
        for e in ('pe', 'act', 'dve', 'pool'):
            self._mksem(e)
        tok = {}
        cnt = {}
        for e in ENGS:
            c = 0
            for i in order[e]:
                nd = self.nodes[i]
                if nd['key'] is not None:
                    cnt[nd['key']] = cnt.get(nd['key'], 0) + 16
                    tok[i] = (nd['key'], cnt[nd['key']])
                else:
                    c += 1
                    tok[i] = (e, c)
        engmap = {'pe': 'tensor', 'act': 'scalar', 'dve': 'vector', 'pool': 'gpsimd', 'sp': 'sync'}
        with nc.Block() as blk:
            for name in ENGS:
                def body(e, name=name):
                    waited = {}
                    for i in order[name]:
                        nd = self.nodes[i]
                        need = {}
                        for d in nd['deps']:
                            dn = self.nodes[d]
                            if name == 'pe' and dn['eng'] == 'pe' and dn['key'] is None:
                                continue
                            s, v = tok[d]
                            if v > need.get(s, 0):
                                need[s] = v
                        for s, v in need.items():
                            if v > waited.get(s, 0):
                                waited[s] = v
                                e.wait_ge(self.sem[s], v)
                        for fn in nd['fns'][:-1]:
                            fn(e)
                        ins = nd['fns'][-1](e)
                        s, _ = tok[i]
                        ins.then_inc(self.sem[s], 16 if nd['key'] is not None else 1)
                    if name == 'sp':
                        for k, v in cnt.items():
                            if k.startswith("st_"):
                                e.wait_ge(self.sem[k], v)
                getattr(blk, engmap[name])(body)


def build(SEQ, dbg=None):
    dbg = dbg or {}
    nc = bass.Bass("TRN2", target_bir_lowering=False)
    NPR = NSEQ_P * SEQ
    NSR = NSEQ_S * LS
    NROW = NPR + NSR

    def din(name, shape, dt=F32):
        return nc.dram_tensor(name, shape, dt, kind="ExternalInput").ap()

    def dout(name, shape, dt=F32):
        return nc.dram_tensor(name, shape, dt, kind="ExternalOutput").ap()

    xp = din("xp", [NPR, D])
    xs = din("xs", [NSR, D])
    cconv = din("cconv", [NSEQ_S, 128, 32, 3])
    sssm = din("sssm", [NSEQ_S, 128, DI])
    w_a_in = din("w_a_in", [128, 8, A_PROJ])
    w_a_out = din("w_a_out", [128, 16, D])
    w_b_in = din("w_b_in", [128, 8, B_PROJ])
    w_b_out = din("w_b_out", [128, 16, D])
    cw_d = din("cw", [128, 32, 4])
    cb_d = din("cb", [128, 32])
    anw_d = din("anw", [128, 16])
    nw_d = din("nw", [2, D])
    fnw_d = din("fnw", [D])
    a3_d = din("a3", [3, NH])
    lnw_d = din("lnw", [DI])
    lnb_d = din("lnb", [DI])
    wsp_d = din("wsp", [128, 8, 128])
    bsp_d = din("bsp", [128, 8])
    ident_d = din("ident", [128, 128])
    mle_d = din("mle", [128, 128])
    ugt_d = din("ugt", [128, 128])

    yp = dout("yp", [NPR, D])
    ys = dout("ys", [NSR, D])
    ncp = dout("ncp", [NSEQ_P, 128, 32, 3])
    nsp = dout("nsp", [NSEQ_P, 128, DI])
    ncs = dout("ncs", [NSEQ_S, 128, 32, 3])
    nss = dout("nss", [NSEQ_S, 128, DI])
    nvs = dout("nvs", [NSR, DI])

    gn_scr = nc.dram_tensor("gn_scr", [NROW, DI], BF16, kind="Internal").ap()
    h1_scr = nc.dram_tensor("h1_scr", [NROW, D], F32, kind="Internal").ap()

    es = ExitStack()
    with es:
        def sb(name, shape, dt):
            return es.enter_context(nc.sbuf_tensor("sb_" + name, shape, dt))

        P = Prog(nc, es)
        PS = [es.enter_context(nc.psum_tensor(f"psum{i}", [128, 512], F32)) for i in range(8)]
        psi = [0]

        def bank():
            i = psi[0] % 8
            psi[0] += 1
            return PS[i], f"ps{i}"

        Win = sb("Win", [128, 8, A_PROJ], BF16)
        ident = sb("ident", [128, 128], BF16)
        mle = sb("mle", [128, 128], BF16)
        ugt = sb("ugt", [128, 128], BF16)
        ones_bf = sb("ones_bf", [128, 128], BF16)
        nw_bc = sb("nw_bc", [128, D], F32)
        cw = sb("cw", [128, 32, 4], F32)
        cb = sb("cb", [128, 32], F32)
        anw = sb("anw", [128, 16], F32)
        a3 = sb("a3", [128, 3, NH], F32)
        A_bc = sb("A_bc", [128, NH], F32)
        mhalf = sb("mhalf", [128, 1], F32)
        bar_t = sb("bar_t", [128, 1], F32)

        WO = [None]

        def load_weights(w_in_d, ncols, w_out_d, tag):
            c0 = 0
            i = 0
            while c0 < ncols:
                c1 = min(c0 + 512, ncols)
                P.dma('pool', f"w{i}", (lambda e, c0=c0, c1=c1: e.dma_start(out=Win[:, :, c0:c1], in_=w_in_d[:, :, c0:c1])),
                      writes=[f"win{i}"])
                c0 = c1
                i += 1
            if w_out_d is not None:
                load_wout(w_out_d)

        def load_wout(w_out_d):
            for i in range(4):
                P.dma('pool', f"wo{i}", (lambda e, i=i: e.dma_start(out=WO[0][:, 4 * i:4 * i + 4, :], in_=w_out_d[:, 4 * i:4 * i + 4, :])),
                      writes=[f"wout{i}"])

        def win_keys(c0, c1):
            return [f"win{i}" for i in range(c0 // 512, (c1 - 1) // 512 + 1)]

        for t, d_ in ((ident, ident_d), (mle, mle_d), (ugt, ugt_d)):
            P.dma('pool', "c_const", (lambda e, t=t, d_=d_: e.dma_start(out=t[:], in_=d_)), writes=["const"])
        P.dma('sp', "c_nw", lambda e: e.dma_start(out=nw_bc[:], in_=nw_d[0].partition_broadcast(128)), writes=["nw_bc"])
        P.dma('sp', "c_cw", lambda e: e.dma_start(out=cw[:], in_=cw_d), writes=["cwk"])
        P.dma('sp', "c_cb", lambda e: e.dma_start(out=cb[:], in_=cb_d), writes=["cbk"])
        P.dma('sp', "c_anw", lambda e: e.dma_start(out=anw[:], in_=anw_d), writes=["anwk"])
        P.dma('sp', "c_a3", lambda e: e.dma_start(out=a3[:].rearrange("p a h -> p (a h)"),
                                                  in_=a3_d.rearrange("a h -> (a h)").partition_broadcast(128)), writes=["a3k"])
        P.op('pool', lambda e: e.memset(mhalf[:], -0.5), writes=["mhalf"])
        P.op('pool', lambda e: e.memset(ones_bf[:], 1.0), writes=["ones"])
        P.op('act', lambda e: e.activation(out=A_bc[:], in_=a3[:, 1, :], func=AF.Exp), reads=["a3k"], writes=["A_bc"])
        P.op('dve', lambda e: e.tensor_scalar(out=A_bc[:], in0=A_bc[:], scalar1=-1.0, scalar2=None, op0=ALU.mult),
             reads=["A_bc"], writes=["A_bc"])
        load_weights(w_a_in, A_PROJ, None, "a")

        XIN = [sb("xin0", [128, D], F32), sb("xin1", [128, D], F32)]
        xn = sb("xn", [128, D], BF16)
        hnT = sb("hnT", [128, 8, 128], BF16)
        ss = sb("ss", [128, 4], F32)
        rs = sb("rs", [128, 4], F32)

        def rstd_from(sum_ap, n, out_ap, RE, ksum, kout):
            P.op('dve', lambda e: e.tensor_scalar(out=sum_ap, in0=sum_ap, scalar1=1.0 / n, scalar2=EPS, op0=ALU.mult, op1=ALU.add),
                 reads=[ksum], writes=[ksum])
            w = out_ap.shape[-1]
            P.op('pool', lambda e: e.tensor_tensor(out=out_ap, in0=sum_ap, in1=mhalf[:RE, 0:1].to_broadcast([RE, w]), op=ALU.pow),
                 reads=[ksum, "mhalf"], writes=[kout])

        RE = 128

        def drain(g):
            for _ in g:
                pass

        def interleave(g1, g2):
            a1 = a2 = True
            while a1 or a2:
                if a1:
                    a1 = next(g1, 'end') != 'end'
                if a2:
                    a2 = next(g2, 'end') != 'end'

        def prenorm(R, xin, kx):
            P.op('pool', lambda e: e.memset(ss[:RE, 0:1], 0.0), writes=["ss0"])
            P.op('act', lambda e: e.activation(out=xn[:RE, :], in_=xin[:RE, :], func=AF.Square, accum_out=ss[:RE, 0:1]),
                 reads=[kx, "ss0"], writes=["xn", "ss0"])
            rstd_from(ss[:RE, 0:1], D, rs[:RE, 0:1], RE, "ss0", "rs0")
            P.op('dve', lambda e: e.scalar_tensor_tensor(out=xn[:RE, :], in0=xin[:RE, :], scalar=rs[:RE, 0:1], in1=nw_bc[:RE, :],
                                                          op0=ALU.mult, op1=ALU.mult),
                 reads=[kx, "rs0", "nw_bc"], writes=["xn"])
            for half in range(2):
                ps, kp = bank()
                for j in range(4):
                    kc = half * 4 + j
                    P.op('pe', (lambda e, ps=ps, j=j, kc=kc: e.matmul(ps[:, j * 128:j * 128 + R], lhsT=xn[:R, kc * 128:(kc + 1) * 128],
                                                                      rhs=ident[:R, :R], start=True, stop=True)),
                         reads=["xn", "const"], writes=[kp], inc=(j == 3))
                P.op('act', (lambda e, ps=ps, half=half: e.activation(
                    out=hnT[:, half * 4:half * 4 + 4, :R], in_=ps[:, :].rearrange("p (a t) -> p a t", a=4)[:, :, :R], func=AF.Copy)),
                    reads=[kp], writes=["hnT"])

        esA = ExitStack()
        with esA:
            def sa(name, shape, dt):
                return esA.enter_context(nc.sbuf_tensor("sa_" + name, shape, dt))
            hist = sa("hist", [128, 32, 3], F32)
            RAW = [sa(f"raw{i}", [128, 4, 131], F32) for i in range(2)]
            ACC = [sa(f"acc{i}", [128, 4, 128], F32) for i in range(2)]
            xcx = sa("xcx", [128, 16, 128], BF16)
            x_tm = sa("x_tm", [128, DI], BF16)
            dtr = sa("dtr", [128, NH], F32)
            dt_ = sa("dt", [128, NH], F32)
            ZS = [sa(f"zs{i}", [128, DI], BF16) for i in range(2)]
            XCB = [sa(f"xcb{i}", [128, 16, 128], BF16) for i in range(2)]
            XDT = [sa(f"xdt{i}", [128, DI], BF16) for i in range(2)]
            XDD = [sa(f"xD{i}", [128, DI], BF16) for i in range(2)]
            BTM = [sa(f"B_tm{i}", [128, 1024], BF16) for i in range(2)]
            DTA = [sa(f"dtA{i}", [128, NH], BF16) for i in range(2)]
            DTF = [sa(f"dtF{i}", [128, NH], F32) for i in range(2)]
            xw = sa("xw", [128, DI], BF16)
            cbm = sa("cbm", [128, 8, 128], BF16)
            R1P = [sa(f"r1p{i}", [128, 512], BF16) for i in range(2)]
            MTP = [sa(f"mtp{i}", [128, 512], BF16) for i in range(4)]
            TMP = [sa(f"tmp{i}", [128, 512], F32) for i in range(2)]
            GN = [sa(f"gn{i}", [128, DI], BF16) for i in range(2)]
            hT = sa("hT", [128, DI], F32)
            hT_bf = sa("hT_bf", [128, DI], BF16)
            ela = sa("ela", [128, 2 * NH], F32)
            elast = sa("elast", [128, NH], F32)
            gss = sa("gss", [128, 8], F32)
            grs = sa("grs", [128, 8], F32)
            junk = sa("junk", [128, 256], BF16)

            def load_a(ti, R, xsrc, *a):
                b = ti % 2
                xin, kx = XIN[b], f"xin{b}"
                P.dma('sp', f"ld{b}", (lambda e: e.dma_start(out=xin[:R, :], in_=xsrc)), writes=[kx])

            def gen_X(ti, R, xsrc, grow, first, last, init, outs):
                b = ti % 2
                xin, kx = XIN[b], f"xin{b}"
                zs, xcb, xdt, xD, B_tm, dtA, dtF = ZS[b], XCB[b], XDT[b], XDD[b], BTM[b], DTA[b], DTF[b]
                if first:
                    if init is None:
                        P.op('pool', lambda e: e.memset(hist[:], 0.0), writes=["hist"])
                    else:
                        P.dma('sp', "ld_hist", (lambda e: e.dma_start(out=hist[:], in_=init[0])), writes=["hist"])
                prenorm(R, xin, kx)
                for zb in range(4):
                    ps, kp = bank()
                    for kc in range(8):
                        P.op('pe', (lambda e, ps=ps, kc=kc, zb=zb: e.matmul(ps[:R, :], lhsT=hnT[:, kc, :R], rhs=Win[:, kc, zb * 512:(zb + 1) * 512],
                                                                           start=(kc == 0), stop=(kc == 7))),
                             reads=["hnT"] + win_keys(zb * 512, zb * 512 + 512), writes=[kp], inc=(kc == 7))
                    P.op('act', (lambda e, ps=ps, zb=zb: e.activation(out=zs[:RE, zb * 512:(zb + 1) * 512], in_=ps[:RE, :], func=AF.Silu)),
                         reads=[kp], writes=[f"zs{b}_{zb}"])
                ps, kp = bank()
                for kc in range(8):
                    P.op('pe', (lambda e, ps=ps, kc=kc: e.matmul(ps[:R, 0:NH], lhsT=hnT[:, kc, :R], rhs=Win[:, kc, 6144:6176],
                                                                 start=(kc == 0), stop=(kc == 7))),
                         reads=["hnT", "win12"], writes=[kp], inc=(kc == 7))
                P.op('dve', (lambda e, ps=ps: e.tensor_tensor(out=dtr[:RE, :], in0=ps[:RE, 0:NH], in1=a3[:RE, 0, :], op=ALU.add)),
                     reads=[kp, "a3k"], writes=["dtr"])
                P.op('act', lambda e: e.activation(out=dtr[:RE, :], in_=dtr[:RE, :], func=AF.Exp), reads=["dtr"], writes=["dtr"])
                P.op('act', lambda e: e.activation(out=dt_[:RE, :], in_=dtr[:RE, :], func=AF.Ln, bias=1.0), reads=["dtr"], writes=["dt"])
                P.op('dve', lambda e: e.tensor_tensor(out=dtF[:RE, :], in0=dt_[:RE, :], in1=A_bc[:RE, :], op=ALU.mult),
                     reads=["dt", "A_bc"], writes=[f"dtF{b}"])
                P.op('dve', lambda e: e.tensor_copy(out=dtA[:RE, :], in_=dtF[:RE, :]), reads=[f"dtF{b}"], writes=[f"dtA{b}"])
                yield
                pend_silu = [None]
                for g in range(8):
                    ps, kp = bank()
                    for j in range(4):
                        ct = 4 * g + j
                        c0 = DI + ct * 128
                        for kc in range(8):
                            P.op('pe', (lambda e, ps=ps, j=j, kc=kc, c0=c0: e.matmul(ps[:, j * 128:j * 128 + R], lhsT=Win[:, kc, c0:c0 + 128],
                                                                                    rhs=hnT[:, kc, :R], start=(kc == 0), stop=(kc == 7))),
                                 reads=["hnT"] + win_keys(c0, c0 + 128), writes=[kp], inc=(j == 3 and kc == 7))
                    raw, kr = RAW[g % 2], f"raw{g % 2}"
                    acc, ka = ACC[g % 2], f"acc{g % 2}"
                    P.op('pool', (lambda e, raw=raw, g=g: e.tensor_copy(out=raw[:, :, 0:3], in_=hist[:, 4 * g:4 * g + 4, :])),
                         reads=["hist"], writes=[kr])
                    P.op('act', (lambda e, ps=ps, raw=raw: e.activation(out=raw[:, :, 3:3 + R],
                                                                        in_=ps[:, :].rearrange("p (a t) -> p a t", a=4)[:, :, :R], func=AF.Copy)),
                         reads=[kp], writes=[kr])
                    P.op('pool', (lambda e, raw=raw, g=g: e.tensor_copy(out=hist[:, 4 * g:4 * g + 4, :], in_=raw[:, :, R:R + 3])),
                         reads=[kr], writes=["hist"])
                    for j in range(4):
                        ct = 4 * g + j
                        P.op('act', (lambda e, raw=raw, acc=acc, j=j, ct=ct: e.activation(
                            out=acc[:, j, :R], in_=raw[:, j, 3:3 + R], func=AF.Identity, scale=cw[:, ct, 3:4], bias=cb[:, ct:ct + 1])),
                            reads=[kr, "cwk", "cbk"], writes=[ka + f"_{j}"])
                        for k in (2, 1, 0):
                            P.op('dve', (lambda e, raw=raw, acc=acc, j=j, ct=ct, k=k: e.scalar_tensor_tensor(
                                out=acc[:, j, :R], in0=raw[:, j, k:k + R], scalar=cw[:, ct, k:k + 1], in1=acc[:, j, :R],
                                op0=ALU.mult, op1=ALU.add)), reads=[kr, "cwk", ka + f"_{j}"], writes=[ka + f"_{j}"])
                    def silu_of(g, acc=acc, ka=ka):
                        if g < 4:
                            P.op('act', (lambda e: e.activation(out=xcx[:, 4 * g:4 * g + 4, :R], in_=acc[:, :, :R], func=AF.Silu)),
                                 reads=[ka + f"_{j}" for j in range(4)], writes=[f"xcx{g}"])
                        else:
                            P.op('act', (lambda e: e.activation(out=xcb[:, 4 * (g - 4):4 * (g - 4) + 4, :R], in_=acc[:, :, :R], func=AF.Silu)),
                                 reads=[ka + f"_{j}" for j in range(4)], writes=[f"xcb{b}_{g - 4}"])
                    if pend_silu[0] is not None:
                        pend_silu[0]()
                    pend_silu[0] = (lambda g=g, f=silu_of: f(g))
                    if g == 3:
                        yield
                pend_silu[0]()
                pend_silu[0] = None
                if last:
                    P.dma('sp', "st_hist", (lambda e: e.dma_start(out=outs[0], in_=hist[:])), reads=["hist"])
                yield
                for g in range(6):
                    ps, kp = bank()
                    for j in range(4):
                        ct = 4 * g + j
                        src = xcx[:, ct, :R] if g < 4 else xcb[:, ct - 16, :R]
                        ksrc = f"xcx{g}" if g < 4 else f"xcb{b}_{g - 4}"
                        P.op('pe', (lambda e, ps=ps, j=j, src=src: e.matmul(ps[:R, j * 128:(j + 1) * 128], lhsT=src, rhs=ident[:, :],
                                                                           start=True, stop=True)),
                             reads=[ksrc, "const"], writes=[kp], inc=(j == 3))
                    if g < 4:
                        P.op('act', (lambda e, ps=ps, g=g: e.activation(out=x_tm[:RE, g * 512:(g + 1) * 512], in_=ps[:RE, :], func=AF.Copy)),
                             reads=[kp], writes=[f"x_tm{g}"])
                        P.op('dve', (lambda e, g=g: e.tensor_tensor(
                            out=xdt[:RE, g * 512:(g + 1) * 512].rearrange("p (h d) -> p h d", h=8),
                            in0=x_tm[:RE, g * 512:(g + 1) * 512].rearrange("p (h d) -> p h d", h=8),
                            in1=dt_[:RE, 8 * g:8 * g + 8].unsqueeze(2).to_broadcast([RE, 8, HD]), op=ALU.mult)),
                            reads=["dt", f"x_tm{g}"], writes=[f"xdt{b}_{g}"])
                        P.op('pool', (lambda e, g=g: e.tensor_tensor(
                            out=xD[:RE, g * 512:(g + 1) * 512].rearrange("p (h d) -> p h d", h=8),
                            in0=x_tm[:RE, g * 512:(g + 1) * 512].rearrange("p (h d) -> p h d", h=8),
                            in1=a3[:RE, 2, 8 * g:8 * g + 8].unsqueeze(2).to_broadcast([RE, 8, HD]), op=ALU.mult)),
                            reads=[f"x_tm{g}", "a3k"], writes=[f"xD{b}_{g}"])
                    else:
                        P.op('act', (lambda e, ps=ps, g=g: e.activation(out=B_tm[:RE, (g - 4) * 512:(g - 3) * 512], in_=ps[:RE, :], func=AF.Copy)),
                             reads=[kp], writes=[f"B_tm{b}_{g - 4}"])

            def gen_Y(ti, R, xsrc, grow, first, last, init, outs):
                b = ti % 2
                zs, xcb, xdt, xD, B_tm, dtA, dtF = ZS[b], XCB[b], XDT[b], XDD[b], BTM[b], DTA[b], DTF[b]
                kB = lambda g: f"xcb{b}_{g // 4}"
                kC = lambda g: f"xcb{b}_{2 + g // 4}"
                if first:
                    if init is None:
                        P.op('pool', lambda e: e.memset(hT[:], 0.0), writes=["hT"])
                        P.op('pool', lambda e: e.memset(hT_bf[:], 0.0), writes=["hT_bf"])
                    else:
                        P.dma('sp', "ld_hT", (lambda e: e.dma_start(out=hT[:], in_=init[1])), writes=["hT"])
                        P.op('act', lambda e: e.activation(out=hT_bf[:], in_=hT[:], func=AF.Copy), reads=["hT"], writes=["hT_bf"])
                ps, kp = bank()
                P.op('pe', (lambda e, ps=ps: e.matmul(ps[:R, 0:NH], lhsT=mle[:R, :R], rhs=dtA[:R, :], start=True, stop=True)),
                     reads=[f"dtA{b}", "const"], writes=[kp], inc=False)
                P.op('pe', (lambda e, ps=ps: e.matmul(ps[:R, NH:2 * NH], lhsT=ugt[:R, :R], rhs=dtA[:R, :], start=True, stop=True)),
                     reads=[f"dtA{b}", "const"], writes=[kp], inc=False)
                P.op('pe', (lambda e, ps=ps: e.matmul(ps[:, 2 * NH:3 * NH], lhsT=ones_bf[:R, :], rhs=dtA[:R, :], start=True, stop=True)),
                     reads=[f"dtA{b}", "ones"], writes=[kp])
                P.op('act', (lambda e, ps=ps: e.activation(out=ela[:RE, :], in_=ps[:RE, 0:2 * NH], func=AF.Exp)), reads=[kp], writes=["ela"])
                P.op('act', (lambda e, ps=ps: e.activation(out=elast[:, :], in_=ps[:, 2 * NH:3 * NH], func=AF.Exp)), reads=[kp], writes=["elast"])
                for bq in range(4):
                    P.op('pool', (lambda e, bq=bq: e.tensor_tensor(
                        out=hT[:, bq * 512:(bq + 1) * 512].rearrange("p (h d) -> p h d", h=8),
                        in0=hT[:, bq * 512:(bq + 1) * 512].rearrange("p (h d) -> p h d", h=8),
                        in1=elast[:, 8 * bq:8 * bq + 8].unsqueeze(2).to_broadcast([128, 8, HD]), op=ALU.mult)),
                        reads=[f"hT{bq}", "hT", "elast"], writes=[f"hT{bq}", "hT"])
                P.op('pool', lambda e: e.tensor_tensor(
                    out=xw[:RE, :].rearrange("p (h d) -> p h d", h=NH), in0=xdt[:RE, :].rearrange("p (h d) -> p h d", h=NH),
                    in1=ela[:RE, NH:2 * NH].unsqueeze(2).to_broadcast([RE, NH, HD]), op=ALU.mult),
                    reads=[f"xdt{b}_{g}" for g in range(4)] + ["ela"], writes=["xw"])
                for half in range(2):
                    ps, kp = bank()
                    for j in range(4):
                        g = half * 4 + j
                        P.op('pe', (lambda e, ps=ps, j=j, g=g: e.matmul(ps[:R, j * 128:j * 128 + R], lhsT=xcb[:, g, :R], rhs=xcb[:, 8 + g, :R],
                                                                       start=True, stop=True)),
                             reads=[kB(g), kC(g)], writes=[kp], inc=(j == 3))
                    P.op('dve', (lambda e, ps=ps, half=half: e.tensor_tensor(
                        out=cbm[:RE, half * 4:half * 4 + 4, :R], in0=ps[:RE, :].rearrange("p (a t) -> p a t", a=4)[:, :, :R],
                        in1=mle[:RE, :R].unsqueeze(1).to_broadcast([RE, 4, R]), op=ALU.mult)),
                        reads=[kp, "const"], writes=[f"cbm{half}"])
                hp = min(NH, 512 // R)
                piece_buf = {}

                def make_piece(pi):
                    r1, kr1 = R1P[pi % 2], f"r1p{pi % 2}"
                    mt, kmt = MTP[pi % 4], f"mtp{pi % 4}"
                    h0 = pi * hp
                    P.op('pool', (lambda e: e.tensor_tensor(
                        out=r1[:RE, :hp * R].rearrange("p (h t) -> p h t", h=hp),
                        in0=dtA[:RE, h0:h0 + hp].unsqueeze(2).to_broadcast([RE, hp, R]),
                        in1=mle[:RE, :R].unsqueeze(1).to_broadcast([RE, hp, R]), op=ALU.mult)),
                        reads=[f"dtA{b}", "const"], writes=[kr1])
                    ps, kp = bank()
                    P.op('pe', (lambda e: e.matmul(ps[:R, :hp * R], lhsT=ugt[:R, :R], rhs=r1[:R, :hp * R], start=True, stop=True)),
                         reads=[kr1, "const"], writes=[kp])
                    P.op('act', (lambda e: e.activation(out=mt[:RE, :hp * R], in_=ps[:RE, :hp * R], func=AF.Exp)), reads=[kp], writes=[kmt])
                    for gi in range(hp // 4):
                        g = h0 // 4 + gi
                        P.op('pool', (lambda e, gi=gi, g=g: e.tensor_tensor(
                            out=mt[:RE, gi * 4 * R:(gi + 1) * 4 * R].rearrange("p (h t) -> p h t", h=4),
                            in0=mt[:RE, gi * 4 * R:(gi + 1) * 4 * R].rearrange("p (h t) -> p h t", h=4),
                            in1=cbm[:RE, g, :R].unsqueeze(1).to_broadcast([RE, 4, R]), op=ALU.mult)),
                            reads=[kmt, f"cbm{g // 4}"], writes=[kmt])
                    piece_buf[pi] = (mt, kmt)

                gnb, kg = GN[b], f"gn{b}"
                def pieces_for(q):
                    for h in range(8 * q, 8 * q + 8):
                        if h // hp not in piece_buf:
                            make_piece(h // hp)
                pieces_for(0)
                pend_tail = [None]
                for q in range(4):
                    if q + 1 < 4:
                        pieces_for(q + 1)
                    psy, kpy = bank()
                    P.op('pe', (lambda e, psy=psy, q=q: e.matmul(psy[:R, :], lhsT=ident[:R, :R], rhs=xD[:R, q * 512:(q + 1) * 512],
                                                                 start=True, stop=False, skip_group_check=True)),
                         reads=[f"xD{b}_{q}", "const"], writes=[kpy], inc=False)
                    for hl in range(8):
                        h = 8 * q + hl
                        pi = h // hp
                        if pi not in piece_buf:
                            make_piece(pi)
                        mt, kmt = piece_buf[pi]
                        ho = (h - pi * hp) * R
                        P.op('pe', (lambda e, psy=psy, hl=hl, h=h, mt=mt, ho=ho: e.matmul(
                            psy[:R, hl * HD:(hl + 1) * HD], lhsT=mt[:R, ho:ho + R], rhs=xdt[:R, h * HD:(h + 1) * HD],
                            start=False, stop=True, skip_group_check=True)),
                            reads=[kmt, f"xdt{b}_{q}"], writes=[kpy], inc=(hl == 7))
                    pss, kps = bank()
                    for gi in range(2):
                        g = 2 * q + gi
                        P.op('pe', (lambda e, pss=pss, gi=gi, g=g: e.matmul(pss[:R, gi * 256:(gi + 1) * 256], lhsT=xcb[:, 8 + g, :R],
                                                                           rhs=hT_bf[:, g * 256:(g + 1) * 256], start=True, stop=True)),
                             reads=[kC(g), "hT_bf"], writes=[kps], inc=(gi == 1))
                    tmp, kt = TMP[q % 2], f"tmp{q % 2}"
                    P.op('dve', (lambda e, pss=pss, tmp=tmp, q=q: e.tensor_tensor(
                        out=tmp[:RE, :].rearrange("p (h d) -> p h d", h=8), in0=pss[:RE, :].rearrange("p (h d) -> p h d", h=8),
                        in1=ela[:RE, 8 * q:8 * q + 8].unsqueeze(2).to_broadcast([RE, 8, HD]), op=ALU.mult)),
                        reads=[kps, "ela"], writes=[kt])
                    P.op('dve', (lambda e, psy=psy, tmp=tmp: e.tensor_tensor(out=tmp[:RE, :], in0=psy[:RE, :], in1=tmp[:RE, :], op=ALU.add)),
                         reads=[kpy, kt], writes=[kt])
                    P.op('pool', (lambda e, tmp=tmp, q=q: e.tensor_tensor(out=tmp[:RE, :], in0=tmp[:RE, :], in1=zs[:RE, q * 512:(q + 1) * 512], op=ALU.mult)),
                         reads=[kt, f"zs{b}_{q}"], writes=[kt])
                    def tail(q=q, tmp=tmp, kt=kt):
                        P.op('pool', (lambda e: e.memset(gss[:RE, 2 * q:2 * q + 2], 0.0)), writes=[f"gss{q}"])
                        for gi in range(2):
                            P.op('act', (lambda e, gi=gi: e.activation(out=junk[:RE, :], in_=tmp[:RE, gi * 256:(gi + 1) * 256], func=AF.Square,
                                                                      accum_out=gss[:RE, 2 * q + gi:2 * q + gi + 1])),
                                 reads=[kt, f"gss{q}"], writes=[f"gss{q}", "junk"])
                        rstd_from(gss[:RE, 2 * q:2 * q + 2], 256, grs[:RE, 2 * q:2 * q + 2], RE, f"gss{q}", f"grs{q}")
                        P.op('pool', (lambda e: e.tensor_tensor(
                            out=gnb[:RE, q * 512:(q + 1) * 512].rearrange("p (g c) -> p g c", g=2), in0=tmp[:RE, :].rearrange("p (g c) -> p g c", g=2),
                            in1=grs[:RE, 2 * q:2 * q + 2].unsqueeze(2).to_broadcast([RE, 2, 256]), op=ALU.mult)),
                            reads=[kt, f"grs{q}"], writes=[kg])
                    if pend_tail[0] is not None:
                        pend_tail[0]()
                    pend_tail[0] = tail
                    if q == 1:
                        yield
                pend_tail[0]()
                P.dma('sp', f"st_gn{b}", (lambda e: e.dma_start(out=gn_scr[grow:grow + R, :], in_=gnb[:R, :])), reads=[kg], writes=["gn_scr"])
                yield
                for bq in range(4):
                    ps, kp = bank()
                    for gi in range(2):
                        g = 2 * bq + gi
                        P.op('pe', (lambda e, ps=ps, gi=gi, g=g: e.matmul(ps[:, gi * 256:(gi + 1) * 256], lhsT=B_tm[:R, g * 128:(g + 1) * 128],
                                                                         rhs=xw[:R, g * 256:(g + 1) * 256], start=True, stop=True)),
                             reads=[f"B_tm{b}_{g // 4}", "xw"], writes=[kp], inc=(gi == 1))
                    P.op('dve', (lambda e, ps=ps, bq=bq: e.tensor_tensor(out=hT[:, bq * 512:(bq + 1) * 512], in0=ps[:, :],
                                                                        in1=hT[:, bq * 512:(bq + 1) * 512], op=ALU.add)),
                         reads=[kp, f"hT{bq}"], writes=[f"hT{bq}"])
                    P.op('act', (lambda e, bq=bq: e.activation(out=hT_bf[:, bq * 512:(bq + 1) * 512], in_=hT[:, bq * 512:(bq + 1) * 512], func=AF.Copy)),
                         reads=[f"hT{bq}"], writes=["hT_bf"])
                if last:
                    P.dma('sp', "st_hT", (lambda e: e.dma_start(out=outs[1], in_=hT[:])), reads=["hT"] + [f"hT{i}" for i in range(4)], writes=[])

            tiles = []
            npt = SEQ // 128
            for s in range(NSEQ_P):
                for i in range(npt):
                    r0 = s * SEQ + i * 128
                    tiles.append((128, xp[r0:r0 + 128, :], r0, i == 0, i == npt - 1, None, (ncp[s], nsp[s])))
            for j in range(NSEQ_S):
                r0 = j * LS
                tiles.append((LS, xs[r0:r0 + LS, :], NPR + r0, True, not dbg.get('nolast'), None if dbg.get('noinit') else (cconv[j], sssm[j]), (ncs[j], nss[j])))
            if dbg.get('skipP'):
                tiles = tiles[NSEQ_P * npt:]
            tiles = tiles[:dbg.get('nA', len(tiles))]


            if tiles:
                load_a(0, *tiles[0])
                if len(tiles) > 1:
                    load_a(1, *tiles[1])
                drain(gen_X(0, *tiles[0]))
            for ti, t in enumerate(tiles):
                if ti + 2 < len(tiles):
                    pass
                if ti + 1 < len(tiles):
                    gx = gen_X(ti + 1, *tiles[ti + 1])
                    interleave(gen_Y(ti, *t), gx)
                    if ti + 2 < len(tiles):
                        load_a(ti + 2, *tiles[ti + 2])
                else:
                    drain(gen_Y(ti, *t))

        P.barrier(lambda e: e.memset(bar_t[:], 0.0))
        esBC = ExitStack()
        es.enter_context(esBC)
        Wout = esBC.enter_context(nc.sbuf_tensor("sb_Wout", [128, 16, D], BF16))
        WO[0] = Wout
        load_wout(w_a_out)
        load_weights(w_b_in, B_PROJ, None, "b")
        P.dma('sp', "c_nw", lambda e: e.dma_start(out=nw_bc[:], in_=nw_d[1].partition_broadcast(128)), writes=["nw_bc"])
        esB = ExitStack()
        with esB:
            def sbb(name, shape, dt):
                return esB.enter_context(nc.sbuf_tensor("sbb_" + name, shape, dt))
            NB = 4
            GIN = [sbb(f"gin{i}", [128, DI], BF16) for i in range(NB)]
            XB = [sbb(f"xb{i}", [128, D], F32) for i in range(NB)]
            YT = [sbb(f"yT{i}", [128, 16, 128], BF16) for i in range(2)]

            def load_b(ti, R, xsrc, grow):
                b = ti % NB
                xin, kx = XB[b], f"xb{b}"
                gin, kgi = GIN[b], f"gin{b}"
                P.dma('sp', f"ldx{b}", (lambda e: e.dma_start(out=xin[:R, :], in_=xsrc)), writes=[kx])
                P.dma('sp', f"ldg{b}", (lambda e: e.dma_start(out=gin[:R, :], in_=gn_scr[grow:grow + R, :])), reads=["gn_scr"], writes=[kgi])

            def tile_b(ti, R, xsrc, grow):
                b = ti % NB
                xin, kx = XB[b], f"xb{b}"
                gin, kgi = GIN[b], f"gin{b}"
                yT = YT[ti % 2]
                ky = f"yT{ti % 2}_"
                for g in range(4):
                    ps, kp = bank()
                    for j in range(4):
                        ct = 4 * g + j
                        P.op('pe', (lambda e, ps=ps, j=j, ct=ct: e.matmul(ps[:, j * 128:j * 128 + R], lhsT=gin[:R, ct * 128:(ct + 1) * 128],
                                                                         rhs=ident[:R, :R], start=True, stop=True)),
                             reads=[kgi, "const"], writes=[kp], inc=(j == 3))
                    for j in range(4):
                        ct = 4 * g + j
                        P.op('act', (lambda e, ps=ps, j=j, ct=ct: e.activation(out=yT[:, ct, :R], in_=ps[:, j * 128:j * 128 + R], func=AF.Copy,
                                                                              scale=anw[:, ct:ct + 1])),
                             reads=[kp, "anwk"], writes=[ky + f"{g}"])
                for half in range(2):
                    ps, kp = bank()
                    for ct in range(16):
                        P.op('pe', (lambda e, ps=ps, ct=ct, half=half: e.matmul(ps[:R, :], lhsT=yT[:, ct, :R], rhs=Wout[:, ct, half * 512:(half + 1) * 512],
                                                                               start=(ct == 0), stop=(ct == 15))),
                             reads=[ky + f"{ct // 4}", f"wout{ct // 4}"], writes=[kp], inc=(ct == 15))
                    P.op('dve', (lambda e, ps=ps, half=half: e.tensor_tensor(out=xin[:RE, half * 512:(half + 1) * 512], in0=ps[:RE, :],
                                                                            in1=xin[:RE, half * 512:(half + 1) * 512], op=ALU.add)),
                         reads=[kp, kx], writes=[kx])
                P.dma('sp', f"st_h{b}", (lambda e: e.dma_start(out=h1_scr[grow:grow + R, :], in_=xin[:R, :])), reads=[kx], writes=["h1_scr"])

            tiles = []
            for s in range(NSEQ_P):
                for i in range(SEQ // 128):
                    r0 = s * SEQ + i * 128
                    tiles.append((128, xp[r0:r0 + 128, :], r0))
            for j in range(NSEQ_S):
                r0 = j * LS
                tiles.append((LS, xs[r0:r0 + LS, :], NPR + r0))
            tiles = tiles[:dbg.get('nB', len(tiles))]
            if tiles:
                load_b(0, *tiles[0])
            for ti, t in enumerate(tiles):
                if ti + 1 < len(tiles):
                    load_b(ti + 1, *tiles[ti + 1])
                tile_b(ti, *t)

        P.barrier(lambda e: e.memset(bar_t[:], 0.0))
        load_wout(w_b_out)
        esC = ExitStack()
        with esC:
            def sc(name, shape, dt):
                return esC.enter_context(nc.sbuf_tensor("sc_" + name, shape, dt))
            fnw_bc = sc("fnw_bc", [128, D], F32)
            lnw_bc = sc("lnw_bc", [128, DI], F32)
            lnb_bc = sc("lnb_bc", [128, DI], F32)
            wmT = sc("wmT", [128, 8, 128], BF16)
            bspc = sc("bspc", [128, 8], F32)
            UT = [sc(f"uT{i}", [128, DI], BF16) for i in range(2)]
            szT = sc("szT", [128, DI], BF16)
            v32 = sc("v32", [128, DI], F32)
            VBF = [sc(f"vbf{i}", [128, DI], BF16) for i in range(2)]
            yT2 = sc("yT2", [128, 16, 128], BF16)
            XC = [XIN[0], XIN[1], sc("xin2", [128, D], F32)]
            xn2 = sc("xn2", [128, D], BF16)
            st = sc("st", [128, 4], F32)
            mv = sc("mv", [128, 4], F32)

            P.dma('sp', "c_fnw", lambda e: e.dma_start(out=fnw_bc[:], in_=fnw_d.partition_broadcast(128)), writes=["fnw"])
            P.dma('sp', "c_lnw", lambda e: e.dma_start(out=lnw_bc[:], in_=lnw_d.partition_broadcast(128)), writes=["lnw"])
            P.dma('sp', "c_lnb", lambda e: e.dma_start(out=lnb_bc[:], in_=lnb_d.partition_broadcast(128)), writes=["lnb"])
            P.dma('pool', "c_wsp", lambda e: e.dma_start(out=wmT[:], in_=wsp_d), writes=["wmT"])
            P.dma('sp', "c_bsp", lambda e: e.dma_start(out=bspc[:], in_=bsp_d), writes=["bsp"])
            P.op('pool', lambda e: e.memset(wmT[64:128, :, 0:64], 0.0), reads=["wmT"], writes=["wmT"])

            def load_c(ti, R, grow, *a):
                b = ti % 2
                xin, kx = XC[ti % 3], f"xin{ti % 3}"
                P.dma('sp', f"ld{b}", (lambda e: e.dma_start(out=xin[:R, :], in_=h1_scr[grow:grow + R, :])), reads=["h1_scr"], writes=[kx])

            def gen_CX(ti, R, grow, ydst, vdst):
                b = ti % 2
                xin, kx = XC[ti % 3], f"xin{ti % 3}"
                uT, vbf = UT[b], VBF[b]
                prenorm(R, xin, kx)
                yield
                for (c_base, func, kd) in ((0, AF.Gelu, f"uT{b}_"), (2 * DI, AF.Silu, "szT")):
                    for cb_ in range(4):
                        ps, kp = bank()
                        c0 = c_base + cb_ * 512
                        for kc in range(8):
                            P.op('pe', (lambda e, ps=ps, kc=kc, c0=c0: e.matmul(ps[:R, :], lhsT=hnT[:, kc, :R], rhs=Win[:, kc, c0:c0 + 512],
                                                                               start=(kc == 0), stop=(kc == 7))),
                                 reads=["hnT"] + win_keys(c0, c0 + 512), writes=[kp], inc=(kc == 7))
                        dst = uT if kd != "szT" else szT
                        P.op('act', (lambda e, ps=ps, cb_=cb_, func=func, dst=dst: e.activation(
                            out=dst[:RE, cb_ * 512:(cb_ + 1) * 512], in_=ps[:RE, :], func=func)),
                            reads=[kp], writes=[f"{kd}{cb_}"])
                        if kd == "szT":
                            P.op('pool', (lambda e, cb_=cb_: e.tensor_tensor(out=uT[:RE, cb_ * 512:(cb_ + 1) * 512], in0=uT[:RE, cb_ * 512:(cb_ + 1) * 512],
                                                                           in1=szT[:RE, cb_ * 512:(cb_ + 1) * 512], op=ALU.mult)),
                                 reads=[f"uT{b}_{cb_}", f"szT{cb_}"], writes=[f"uT{b}_{cb_}"])
                        if cb_ % 2 == 1:
                            yield
                P.op('pool', lambda e: e.memset(st[:RE, 0:4], 0.0), writes=["st"])
                for vb in range(4):
                    ps, kp = bank()
                    c0 = DI + vb * 512
                    for kc in range(8):
                        P.op('pe', (lambda e, ps=ps, kc=kc, c0=c0: e.matmul(ps[:R, :], lhsT=hnT[:, kc, :R], rhs=Win[:, kc, c0:c0 + 512],
                                                                           start=(kc == 0), stop=(kc == 7))),
                             reads=["hnT"] + win_keys(c0, c0 + 512), writes=[kp], inc=(kc == 7))
                    P.op('act', (lambda e, ps=ps, vb=vb: e.activation(out=v32[:RE, vb * 512:(vb + 1) * 512], in_=ps[:RE, :], func=AF.Gelu)),
                         reads=[kp], writes=[f"v{vb}"])
                    if vb == 1:
                        yield
                yield
                vk = [f"v{i}" for i in range(4)]
                kvb = f"vbf{b}"
                P.op('act', lambda e: e.activation(out=vbf[:RE, :], in_=v32[:RE, :], func=AF.Copy, accum_out=st[:RE, 0:1]), reads=vk + ["st"], writes=[kvb, "st"])
                P.op('act', lambda e: e.activation(out=vbf[:RE, :], in_=v32[:RE, :], func=AF.Square, accum_out=st[:RE, 1:2]), reads=vk + ["st"], writes=[kvb, "st"])
                P.op('dve', lambda e: e.tensor_scalar(out=mv[:RE, 0:2], in0=st[:RE, 0:2], scalar1=1.0 / DI, scalar2=None, op0=ALU.mult),
                     reads=["st"], writes=["mv"])
                P.op('dve', lambda e: e.tensor_tensor(out=mv[:RE, 2:3], in0=mv[:RE, 0:1], in1=mv[:RE, 0:1], op=ALU.mult), reads=["mv"], writes=["mv"])
                P.op('dve', lambda e: e.tensor_tensor(out=mv[:RE, 2:3], in0=mv[:RE, 1:2], in1=mv[:RE, 2:3], op=ALU.subtract), reads=["mv"], writes=["mv"])
                P.op('dve', lambda e: e.tensor_scalar(out=mv[:RE, 2:3], in0=mv[:RE, 2:3], scalar1=EPS, scalar2=None, op0=ALU.add), reads=["mv"], writes=["mv"])
                P.op('pool', lambda e: e.tensor_tensor(out=mv[:RE, 3:4], in0=mv[:RE, 2:3], in1=mhalf[:RE, 0:1], op=ALU.pow), reads=["mv", "mhalf"], writes=["mv"])
                for vb in range(4):
                    sl = slice(vb * 512, (vb + 1) * 512)
                    kv = [f"v{vb}"]
                    P.op('dve', (lambda e, sl=sl: e.tensor_scalar(out=v32[:RE, sl], in0=v32[:RE, sl], scalar1=mv[:RE, 0:1], scalar2=mv[:RE, 3:4],
                                                                  op0=ALU.subtract, op1=ALU.mult)), reads=kv + ["mv"], writes=kv)
                    if vb % 2 == 0:
                        P.op('pool', (lambda e, sl=sl: e.tensor_tensor(out=v32[:RE, sl], in0=v32[:RE, sl], in1=lnw_bc[:RE, sl], op=ALU.mult)),
                             reads=kv + ["lnw"], writes=kv)
                    else:
                        P.op('dve', (lambda e, sl=sl: e.tensor_tensor(out=v32[:RE, sl], in0=v32[:RE, sl], in1=lnw_bc[:RE, sl], op=ALU.mult)),
                             reads=kv + ["lnw"], writes=kv)
                    P.op('dve', (lambda e, sl=sl: e.tensor_tensor(out=v32[:RE, sl], in0=v32[:RE, sl], in1=lnb_bc[:RE, sl], op=ALU.add)),
                         reads=kv + ["lnb"], writes=kv)
                    P.op('act', (lambda e, sl=sl: e.activation(out=vbf[:RE, sl], in_=v32[:RE, sl], func=AF.Copy)), reads=kv, writes=[kvb + f"_{vb}"])
                if vdst is not None:
                    P.dma('sp', "st_v", (lambda e: e.dma_start(out=vdst, in_=v32[:R, :])), reads=vk)

            def gen_CY(ti, R, grow, ydst, vdst):
                b = ti % 2
                xin, kx = XC[ti % 3], f"xin{ti % 3}"
                uT, vbf = UT[b], VBF[b]
                kvb = f"vbf{b}"
                for g2 in range(4):
                    ps, kp = bank()
                    for j in range(2):
                        g = 2 * g2 + j
                        P.op('pe', (lambda e, ps=ps, j=j, g=g: e.matmul(ps[:R, j * 256:(j + 1) * 256], lhsT=wmT[:R, g, :R],
                                                                       rhs=vbf[:R, g * 256:(g + 1) * 256], start=True, stop=True)),
                             reads=[kvb, kvb + f"_{g // 2}", "wmT"], writes=[kp], inc=(j == 1))
                    for j in range(2):
                        g = 2 * g2 + j
                        P.op('dve', (lambda e, ps=ps, j=j, g=g: e.scalar_tensor_tensor(
                            out=uT[:RE, g * 256:(g + 1) * 256], in0=ps[:RE, j * 256:(j + 1) * 256], scalar=bspc[:RE, g:g + 1],
                            in1=uT[:RE, g * 256:(g + 1) * 256], op0=ALU.add, op1=ALU.mult)),
                            reads=[kp, "bsp", f"uT{b}_{g2}"], writes=[f"uT{b}_{g2}"])
                    if g2 == 1:
                        yield
                yield
                for g in range(4):
                    ps, kp = bank()
                    for j in range(4):
                        ct = 4 * g + j
                        P.op('pe', (lambda e, ps=ps, j=j, ct=ct: e.matmul(ps[:, j * 128:j * 128 + R], lhsT=uT[:R, ct * 128:(ct + 1) * 128],
                                                                         rhs=ident[:R, :R], start=True, stop=True)),
                             reads=[f"uT{b}_{g}", "const"], writes=[kp], inc=(j == 3))
                    P.op('act', (lambda e, ps=ps, g=g: e.activation(out=yT2[:, 4 * g:4 * g + 4, :R],
                                                                    in_=ps[:, :].rearrange("p (a t) -> p a t", a=4)[:, :, :R], func=AF.Copy)),
                         reads=[kp], writes=[f"yT2{g}"])
                yield
                for half in range(2):
                    ps, kp = bank()
                    for ct in range(16):
                        P.op('pe', (lambda e, ps=ps, ct=ct, half=half: e.matmul(ps[:R, :], lhsT=yT2[:, ct, :R], rhs=Wout[:, ct, half * 512:(half + 1) * 512],
                                                                               start=(ct == 0), stop=(ct == 15))),
                             reads=[f"yT2{ct // 4}", f"wout{ct // 4}"], writes=[kp], inc=(ct == 15))
                    P.op('dve', (lambda e, ps=ps, half=half: e.tensor_tensor(out=xin[:RE, half * 512:(half + 1) * 512], in0=ps[:RE, :],
                                                                            in1=xin[:RE, half * 512:(half + 1) * 512], op=ALU.add)),
                         reads=[kp, kx], writes=[kx])
                    yield
                P.op('pool', lambda e: e.memset(ss[:RE, 1:2], 0.0), writes=["ss1"])
                P.op('act', lambda e: e.activation(out=xn2[:RE, :], in_=xin[:RE, :], func=AF.Square, accum_out=ss[:RE, 1:2]),
                     reads=[kx, "ss1"], writes=["xn2", "ss1"])
                rstd_from(ss[:RE, 1:2], D, rs[:RE, 1:2], RE, "ss1", "rs1")
                P.op('dve', lambda e: e.scalar_tensor_tensor(out=xin[:RE, :], in0=xin[:RE, :], scalar=rs[:RE, 1:2], in1=fnw_bc[:RE, :],
                                                              op0=ALU.mult, op1=ALU.mult), reads=[kx, "rs1", "fnw"], writes=[kx])
                P.dma('sp', f"st_y{ti % 3}", (lambda e: e.dma_start(out=ydst, in_=xin[:R, :])), reads=[kx])

            tiles = []
            for s in range(NSEQ_P):
                for i in range(SEQ // 128):
                    r0 = s * SEQ + i * 128
                    tiles.append((128, r0, yp[r0:r0 + 128, :], None))
            for j in range(NSEQ_S):
                r0 = j * LS
                tiles.append((LS, NPR + r0, ys[r0:r0 + LS, :], nvs[r0:r0 + LS, :]))
            tiles = tiles[:dbg.get('nC', len(tiles))]
            if tiles:
                load_c(0, *tiles[0])
                drain(gen_CX(0, *tiles[0]))
            for ti, t in enumerate(tiles):
                if ti + 1 < len(tiles):
                    load_c(ti + 1, *tiles[ti + 1])
                    interleave(gen_CY(ti, *t), gen_CX(ti + 1, *tiles[ti + 1]))
                else:
                    drain(gen_CY(ti, *t))

            P.emit()
    return nc


def _consts():
    k = np.arange(128)
    ident = np.eye(128, dtype=np.float32)
    mle = (k[:, None] <= k[None, :]).astype(np.float32)
    ugt = (k[:, None] > k[None, :]).astype(np.float32)
    return ident, mle, ugt


def _klayout(w):
    K, C = w.shape
    return np.ascontiguousarray(w.reshape(K // 128, 128, C).transpose(1, 0, 2))


def make_in_maps(inputs, n_cores, SEQ):
    f = lambda a: np.ascontiguousarray(np.asarray(a, dtype=np.float32))
    ident, mle, ugt = _consts()
    shared = {
        "w_a_in": _klayout(f(inputs["a_w_in"][0])),
        "w_a_out": _klayout(f(inputs["a_w_out"][0])),
        "w_b_in": _klayout(f(inputs["b_w_in"][0])),
        "w_b_out": _klayout(f(inputs["b_w_out"][0])),
        "cw": np.ascontiguousarray(f(inputs["a_conv_w"][0]).reshape(4, 32, 128).transpose(2, 1, 0)),
        "cb": np.ascontiguousarray(f(inputs["a_conv_b"][0]).reshape(32, 128).T),
        "anw": np.ascontiguousarray(f(inputs["a_norm_w"][0]).reshape(16, 128).T),
        "nw": f(inputs["norm_w"]),
        "fnw": f(inputs["final_norm_w"]),
        "a3": np.ascontiguousarray(np.stack([f(inputs["a_dt_bias"][0]), f(inputs["a_log"][0]), f(inputs["a_d"][0])])),
        "lnw": f(inputs["b_ln_w"][0]),
        "lnb": f(inputs["b_ln_b"][0]),
        "wsp": np.ascontiguousarray(f(inputs["b_w_sp"][0]).transpose(2, 0, 1)),
        "bsp": np.ascontiguousarray(f(inputs["b_b_sp"][0]).T),
        "ident": ident, "mle": mle, "ugt": ugt,
    }
    xp = f(inputs["x_prompt"])
    xs = f(inputs["x_sample"])
    cc = f(inputs["cache_conv"][0])
    sm = f(inputs["state_ssm"][0])
    maps = []
    for c in range(n_cores):
        m = dict(shared)
        m["xp"] = np.ascontiguousarray(xp[NSEQ_P * c:NSEQ_P * (c + 1)].reshape(NSEQ_P * SEQ, D))
        m["xs"] = np.ascontiguousarray(xs[NSEQ_S * c:NSEQ_S * (c + 1)].reshape(NSEQ_S * LS, D))
        ccs = cc[NSEQ_S * c:NSEQ_S * (c + 1)]
        m["cconv"] = np.ascontiguousarray(ccs.reshape(NSEQ_S, 3, 32, 128).transpose(0, 3, 2, 1))
        sms = sm[NSEQ_S * c:NSEQ_S * (c + 1)]
        m["sssm"] = np.ascontiguousarray(sms.reshape(NSEQ_S, DI, NS).transpose(0, 2, 1))
        maps.append(m)
    return maps


def assemble(results, n_cores, SEQ):
    yp = np.concatenate([r["yp"].reshape(NSEQ_P, SEQ, D) for r in results], 0)
    ys = np.concatenate([r["ys"].reshape(NSEQ_S, LS, D) for r in results], 0)

    def conv_back(a):
        n = a.shape[0]
        return np.ascontiguousarray(a.transpose(0, 3, 2, 1).reshape(n, 3, 4096))

    def ssm_back(a):
        n = a.shape[0]
        return np.ascontiguousarray(a.transpose(0, 2, 1).reshape(n, NH, HD, NS))
    ncp = np.concatenate([conv_back(r["ncp"]) for r in results], 0)[None]
    nsp = np.concatenate([ssm_back(r["nsp"]) for r in results], 0)[None]
    ncs = np.concatenate([conv_back(r["ncs"]) for r in results], 0)[None]
    nss = np.concatenate([ssm_back(r["nss"]) for r in results], 0)[None]
    nvs = np.concatenate([r["nvs"].reshape(NSEQ_S, LS, DI) for r in results], 0)[None]
    return tuple(np.asarray(a, dtype=np.float32) for a in (yp, ys, ncp, nsp, ncs, nss, nvs))


def kernel(**inputs):
    n_cores = 8
    SEQ = inputs["x_prompt"].shape[1]
    nc = build(SEQ)
    maps = make_in_maps(inputs, n_cores, SEQ)
    res = run_bass_kernel_spmd(nc, maps, core_ids=list(range(n_cores)))
    return assemble(res.results, n_cores, SEQ)
```

```python
import numpy as np
from contextlib import ExitStack
import concourse.bass as bass
import concourse.mybir as mybir
from concourse.bass_utils import run_bass_kernel_spmd

F32 = mybir.dt.float32
BF16 = mybir.dt.bfloat16
AF = mybir.ActivationFunctionType
ALU = mybir.AluOpType

EPS = 1e-5
D = 1024
DI = 2048
NH = 32
HD = 64
NS = 128
A_PROJ = 6176
B_PROJ = 6144
LS = 16
NSEQ_P = 2
NSEQ_S = 4
ENGS = ('pe', 'act', 'dve', 'pool', 'sp')


class _Probe:
    def __init__(self):
        self.kind, self.n, self.func = None, 0, None

    def _fs(self, ap):
        n = 1
        for d in list(ap.shape)[1:]:
            n *= int(d)
        return n

    def __getattr__(self, name):
        def f(*a, **k):
            self.kind = name
            out = k.get('out', a[0] if a else None)
            if name == 'matmul':
                self.n = self._fs(k.get('rhs', a[2] if len(a) > 2 else out))
            elif name == 'dma_start':
                self.n = self._fs(out) * int(out.shape[0])
            else:
                self.n = self._fs(out)
                two = ('in1' in k) or name == 'scalar_tensor_tensor'
                if two:
                    self.n *= 2
            self.func = k.get('func')
            return self
        return f

    def then_inc(self, *a, **k):
        return self


_TABLE = {}


def _table_of(func):
    if func is None:
        return None
    nm = str(func).split('.')[-1]
    return {'Silu': 'silu', 'Gelu': 'gelu', 'Exp': 'exp', 'Ln': 'exp'}.get(nm)


class Prog:
    LAT = 1.2

    def __init__(self, nc, es):
        self.nc, self.es = nc, es
        self.nodes = []
        self.lastw, self.readers = {}, {}
        self.pend_pe = None
        self.last_dma = {}
        self.sem = {}
        self.bar = None

    def _mksem(self, name):
        self.sem[name] = self.es.enter_context(self.nc.semaphore("s_" + name))

    def barrier(self, fn):
        self._close_pe()
        has_succ = set()
        for nd in self.nodes:
            has_succ |= set(nd['deps'])
        sinks = set(i for i in range(len(self.nodes)) if i not in has_succ)
        nid = len(self.nodes)
        self.nodes.append(dict(eng='pool', fns=[fn], deps=sinks, dur=0.2, table=None, key=None, lat=0.0))
        self.bar = nid

    def _edges(self, reads, writes):
        d = set()
        if self.bar is not None:
            d.add(self.bar)
        for b in reads:
            if b in self.lastw:
                d.add(self.lastw[b])
        for b in writes:
            if b in self.lastw:
                d.add(self.lastw[b])
            d.update(self.readers.get(b, ()))
        return d

    def _touch(self, nid, reads, writes):
        for b in reads:
            self.readers.setdefault(b, set()).add(nid)
        for b in writes:
            self.lastw[b] = nid
            self.readers[b] = set()

    def _cost(self, eng, fn):
        p = _Probe()
        fn(p)
        n = p.n
        if eng == 'pe':
            return max(0.06, n / 2400.0) + (0.05 if n <= 128 else 0.0), None
        if p.kind == 'dma_start':
            return 0.1, None
        if eng == 'act':
            return 0.22 + n / 1400.0, _table_of(p.func)
        if eng == 'dve':
            return 0.07 + n / 960.0, None
        return 0.15 + n / 420.0, None

    def op(self, eng, fn, reads=(), writes=(), inc=True):
        dur, table = self._cost(eng, fn)
        if eng == 'pe':
            if self.pend_pe is None:
                self.pend_pe = dict(eng='pe', fns=[], deps=set(), dur=0.0, table=None, key=None, lat=0.0, reads=[], writes=[])
            g = self.pend_pe
            g['fns'].append(fn)
            g['dur'] += dur
            g['deps'] |= self._edges(reads, writes)
            g['reads'] += list(reads)
            g['writes'] += list(writes)
            if inc:
                self._close_pe()
            return
        nid = len(self.nodes)
        self.nodes.append(dict(eng=eng, fns=[fn], deps=self._edges(reads, writes), dur=dur, table=table, key=None, lat=0.0))
        self._touch(nid, reads, writes)

    def _close_pe(self):
        g = self.pend_pe
        if g is None:
            return
        self.pend_pe = None
        nid = len(self.nodes)
        g['deps'].discard(nid)
        self.nodes.append(g)
        self._touch(nid, g.pop('reads'), g.pop('writes'))

    def dma(self, eng, key, fn, reads=(), writes=()):
        if key not in self.sem:
            self._mksem(key)
        p = _Probe()
        fn(p)
        nid = len(self.nodes)
        deps = self._edges(reads, writes)
        if key in self.last_dma:
            deps.add(self.last_dma[key])
        self.last_dma[key] = nid
        self.nodes.append(dict(eng=eng, fns=[fn], deps=deps, dur=0.15 if eng == 'sp' else 1.0, table=None, key=key,
                               lat=2.0 + p.n * 4 / 150e3))
        self._touch(nid, reads, writes)

    def schedule(self):
        import heapq
        self._close_pe()
        N = len(self.nodes)
        succ = [[] for _ in range(N)]
        ndep = [0] * N
        for i, nd in enumerate(self.nodes):
            nd['deps'] = [d for d in nd['deps'] if d != i]
            ndep[i] = len(nd['deps'])
            for d in nd['deps']:
                succ[d].append(i)
        fin = [0.0] * N
        heaps = {e: [] for e in ENGS}
        free = {e: 0.0 for e in ENGS}
        cur_table = [None]
        order = {e: [] for e in ENGS}

        def ready_time(i):
            nd = self.nodes[i]
            t = 0.0
            for d in nd['deps']:
                dn = self.nodes[d]
                lat = 0.0 if (dn['eng'] == nd['eng'] and dn['key'] is None) else self.LAT
                t = max(t, fin[d] + lat)
            return t

        for i in range(N):
            if ndep[i] == 0:
                heapq.heappush(heaps[self.nodes[i]['eng']], (0.0, i))
        done = 0
        while done < N:
            best = None
            for e in ENGS:
                h = heaps[e]
                if not h:
                    continue
                cands = [h[0]]
                if e == 'act' and len(h) > 1:
                    cands = heapq.nsmallest(8, h)
                for (rt, i) in cands:
                    st = max(free[e], rt)
                    pen = 0.0
                    tb = self.nodes[i]['table']
                    if e == 'act' and tb is not None and tb != cur_table[0]:
                        pen = 1.3
                    key = (st + 2.0 * pen, i)
                    if best is None or key < best[0]:
                        best = (key, e, rt, i, pen)
            (_, e, rt, i, pen) = best
            h = heaps[e]
            if h[0][1] == i:
                heapq.heappop(h)
            else:
                h.remove((rt, i))
                heapq.heapify(h)
            nd = self.nodes[i]
            st = max(free[e], rt) + pen
            free[e] = st + nd['dur']
            fin[i] = free[e] + nd['lat']
            if e == 'act' and nd['table'] is not None:
                cur_table[0] = nd['table']
            order[e].append(i)
            done += 1
            for s_ in succ[i]:
                ndep[s_] -= 1
                if ndep[s_] == 0:
                    heapq.heappush(heaps[self.nodes[s_]['eng']], (ready_time(s_), s_))
        self.order = order
        self.sim_time = max(free.values())
        return order

    def emit(self, final_keys=None):
        nc = self.nc
        order = self.schedule()
        for e in ('pe', 'act', 'dve', 'pool'):
            self._mksem(e)
        tok = {}
        cnt = {}
        for e in ENGS:
            c = 0
            for i in order[e]:
                nd = self.nodes[i]
                if nd['key'] is not None:
                    cnt[nd['key']] = cnt.get(nd['key'], 0) + 16
                    tok[i] = (nd['key'], cnt[nd['key']])
                else:
                    c += 1
                    tok[i] = (e, c)
        engmap = {'pe': 'tensor', 'act': 'scalar', 'dve': 'vector', 'pool': 'gpsimd', 'sp': 'sync'}
        with nc.Block() as blk:
            for name in ENGS:
                def body(e, name=name):
                    waited = {}
                    for i in order[name]:
                        nd = self.nodes[i]
                        need = {}
                        for d in nd['deps']:
                            dn = self.nodes[d]
                            if name == 'pe' and dn['eng'] == 'pe' and dn['key'] is None:
                                continue
                            s, v = tok[d]
                            if v > need.get(s, 0):
                                need[s] = v
                        for s, v in need.items():
                            if v > waited.get(s, 0):
                                waited[s] = v
                                e.wait_ge(self.sem[s], v)
                        for fn in nd['fns'][:-1]:
                            fn(e)
                        ins = nd['fns'][-1](e)
                        s, _ = tok[i]
                        ins.then_inc(self.sem[s], 16 if nd['key'] is not None else 1)
                    if name == 'sp':
                        for k, v in cnt.items():
                            if k.startswith("st_"):
                                e.wait_ge(self.sem[k], v)
                getattr(blk, engmap[name])(body)


def build(SEQ, dbg=None):
    dbg = dbg or {}
    nc = bass.Bass("TRN2", target_bir_lowering=False)
    NPR = NSEQ_P * SEQ
    NSR = NSEQ_S * LS
    NROW = NPR + NSR

    def din(name, shape, dt=F32):
        return nc.dram_tensor(name, shape, dt, kind="ExternalInput").ap()

    def dout(name, shape, dt=F32):
        return nc.dram_tensor(name, shape, dt, kind="ExternalOutput").ap()

    xp = din("xp", [NPR, D])
    xs = din("xs", [NSR, D])
    cconv = din("cconv", [NSEQ_S, 128, 32, 3])
    sssm = din("sssm", [NSEQ_S, 128, DI])
    w_a_in = din("w_a_in", [128, 8, A_PROJ])
    w_a_out = din("w_a_out", [128, 16, D])
    w_b_in = din("w_b_in", [128, 8, B_PROJ])
    w_b_out = din("w_b_out", [128, 16, D])
    cw_d = din("cw", [128, 32, 4])
    cb_d = din("cb", [128, 32])
    anw_d = din("anw", [128, 16])
    nw_d = din("nw", [2, D])
    fnw_d = din("fnw", [D])
    a3_d = din("a3", [3, NH])
    lnw_d = din("lnw", [DI])
    lnb_d = din("lnb", [DI])
    wsp_d = din("wsp", [128, 8, 128])
    bsp_d = din("bsp", [128, 8])
    ident_d = din("ident", [128, 128])
    mle_d = din("mle", [128, 128])
    ugt_d = din("ugt", [128, 128])

    yp = dout("yp", [NPR, D])
    ys = dout("ys", [NSR, D])
    ncp = dout("ncp", [NSEQ_P, 128, 32, 3])
    nsp = dout("nsp", [NSEQ_P, 128, DI])
    ncs = dout("ncs", [NSEQ_S, 128, 32, 3])
    nss = dout("nss", [NSEQ_S, 128, DI])
    nvs = dout("nvs", [NSR, DI])

    gn_scr = nc.dram_tensor("gn_scr", [NROW, DI], BF16, kind="Internal").ap()
    h1_scr = nc.dram_tensor("h1_scr", [NROW, D], F32, kind="Internal").ap()

    es = ExitStack()
    with es:
        def sb(name, shape, dt):
            return es.enter_context(nc.sbuf_tensor("sb_" + name, shape, dt))

        P = Prog(nc, es)
        PS = [es.enter_context(nc.psum_tensor(f"psum{i}", [128, 512], F32)) for i in range(8)]
        psi = [0]

        def bank():
            i = psi[0] % 8
            psi[0] += 1
            return PS[i], f"ps{i}"

        Win = sb("Win", [128, 8, A_PROJ], BF16)
        ident = sb("ident", [128, 128], BF16)
        mle = sb("mle", [128, 128], BF16)
        ugt = sb("ugt", [128, 128], BF16)
        ones_bf = sb("ones_bf", [128, 128], BF16)
        nw_bc = sb("nw_bc", [128, D], F32)
        cw = sb("cw", [128, 32, 4], F32)
        cb = sb("cb", [128, 32], F32)
        anw = sb("anw", [128, 16], F32)
        a3 = sb("a3", [128, 3, NH], F32)
        A_bc = sb("A_bc", [128, NH], F32)
        mhalf = sb("mhalf", [128, 1], F32)
        bar_t = sb("bar_t", [128, 1], F32)

        WO = [None]

        def load_weights(w_in_d, ncols, w_out_d, tag):
            c0 = 0
            i = 0
            while c0 < ncols:
                c1 = min(c0 + 512, ncols)
                P.dma('pool', f"w{i}", (lambda e, c0=c0, c1=c1: e.dma_start(out=Win[:, :, c0:c1], in_=w_in_d[:, :, c0:c1])),
                      writes=[f"win{i}"])
                c0 = c1
                i += 1
            if w_out_d is not None:
                load_wout(w_out_d)

        def load_wout(w_out_d):
            for i in range(4):
                P.dma('pool', f"wo{i}", (lambda e, i=i: e.dma_start(out=WO[0][:, 4 * i:4 * i + 4, :], in_=w_out_d[:, 4 * i:4 * i + 4, :])),
                      writes=[f"wout{i}"])

        def win_keys(c0, c1):
            return [f"win{i}" for i in range(c0 // 512, (c1 - 1) // 512 + 1)]

        for t, d_ in ((ident, ident_d), (mle, mle_d), (ugt, ugt_d)):
            P.dma('pool', "c_const", (lambda e, t=t, d_=d_: e.dma_start(out=t[:], in_=d_)), writes=["const"])
        P.dma('sp', "c_nw", lambda e: e.dma_start(out=nw_bc[:], in_=nw_d[0].partition_broadcast(128)), writes=["nw_bc"])
        P.dma('sp', "c_cw", lambda e: e.dma_start(out=cw[:], in_=cw_d), writes=["cwk"])
        P.dma('sp', "c_cb", lambda e: e.dma_start(out=cb[:], in_=cb_d), writes=["cbk"])
        P.dma('sp', "c_anw", lambda e: e.dma_start(out=anw[:], in_=anw_d), writes=["anwk"])
        P.dma('sp', "c_a3", lambda e: e.dma_start(out=a3[:].rearrange("p a h -> p (a h)"),
                                                  in_=a3_d.rearrange("a h -> (a h)").partition_broadcast(128)), writes=["a3k"])
        P.op('pool', lambda e: e.memset(mhalf[:], -0.5), writes=["mhalf"])
        P.op('pool', lambda e: e.memset(ones_bf[:], 1.0), writes=["ones"])
        P.op('act', lambda e: e.activation(out=A_bc[:], in_=a3[:, 1, :], func=AF.Exp), reads=["a3k"], writes=["A_bc"])
        P.op('dve', lambda e: e.tensor_scalar(out=A_bc[:], in0=A_bc[:], scalar1=-1.0, scalar2=None, op0=ALU.mult),
             reads=["A_bc"], writes=["A_bc"])
        load_weights(w_a_in, A_PROJ, None, "a")

        XIN = [sb("xin0", [128, D], F32), sb("xin1", [128, D], F32)]
        xn = sb("xn", [128, D], BF16)
        hnT = sb("hnT", [128, 8, 128], BF16)
        ss = sb("ss", [128, 4], F32)
        rs = sb("rs", [128, 4], F32)

        def rstd_from(sum_ap, n, out_ap, RE, ksum, kout):
            P.op('dve', lambda e: e.tensor_scalar(out=sum_ap, in0=sum_ap, scalar1=1.0 / n, scalar2=EPS, op0=ALU.mult, op1=ALU.add),
                 reads=[ksum], writes=[ksum])
            w = out_ap.shape[-1]
            P.op('pool', lambda e: e.tensor_tensor(out=out_ap, in0=sum_ap, in1=mhalf[:RE, 0:1].to_broadcast([RE, w]), op=ALU.pow),
                 reads=[ksum, "mhalf"], writes=[kout])

        RE = 128

        def drain(g):
            for _ in g:
                pass

        def interleave(g1, g2):
            a1 = a2 = True
            while a1 or a2:
                if a1:
                    a1 = next(g1, 'end') != 'end'
                if a2:
                    a2 = next(g2, 'end') != 'end'

        def prenorm(R, xin, kx):
            P.op('pool', lambda e: e.memset(ss[:RE, 0:1], 0.0), writes=["ss0"])
            P.op('act', lambda e: e.activation(out=xn[:RE, :], in_=xin[:RE, :], func=AF.Square, accum_out=ss[:RE, 0:1]),
                 reads=[kx, "ss0"], writes=["xn", "ss0"])
            rstd_from(ss[:RE, 0:1], D, rs[:RE, 0:1], RE, "ss0", "rs0")
            P.op('dve', lambda e: e.scalar_tensor_tensor(out=xn[:RE, :], in0=xin[:RE, :], scalar=rs[:RE, 0:1], in1=nw_bc[:RE, :],
                                                          op0=ALU.mult, op1=ALU.mult),
                 reads=[kx, "rs0", "nw_bc"], writes=["xn"])
            for half in range(2):
                ps, kp = bank()
                for j in range(4):
                    kc = half * 4 + j
                    P.op('pe', (lambda e, ps=ps, j=j, kc=kc: e.matmul(ps[:, j * 128:j * 128 + R], lhsT=xn[:R, kc * 128:(kc + 1) * 128],
                                                                      rhs=ident[:R, :R], start=True, stop=True)),
                         reads=["xn", "const"], writes=[kp], inc=(j == 3))
                P.op('act', (lambda e, ps=ps, half=half: e.activation(
                    out=hnT[:, half * 4:half * 4 + 4, :R], in_=ps[:, :].rearrange("p (a t) -> p a t", a=4)[:, :, :R], func=AF.Copy)),
                    reads=[kp], writes=["hnT"])

        esA = ExitStack()
        with esA:
            def sa(name, shape, dt):
                return esA.enter_context(nc.sbuf_tensor("sa_" + name, shape, dt))
            hist = sa("hist", [128, 32, 3], F32)
            RAW = [sa(f"raw{i}", [128, 4, 131], F32) for i in range(2)]
            ACC = [sa(f"acc{i}", [128, 4, 128], F32) for i in range(2)]
            xcx = sa("xcx", [128, 16, 128], BF16)
            x_tm = sa("x_tm", [128, DI], BF16)
            dtr = sa("dtr", [128, NH], F32)
            dt_ = sa("dt", [128, NH], F32)
            ZS = [sa(f"zs{i}", [128, DI], BF16) for i in range(2)]
            XCB = [sa(f"xcb{i}", [128, 16, 128], BF16) for i in range(2)]
            XDT = [sa(f"xdt{i}", [128, DI], BF16) for i in range(2)]
            XDD = [sa(f"xD{i}", [128, DI], BF16) for i in range(2)]
            BTM = [sa(f"B_tm{i}", [128, 1024], BF16) for i in range(2)]
            DTA = [sa(f"dtA{i}", [128, NH], BF16) for i in range(2)]
            DTF = [sa(f"dtF{i}", [128, NH], F32) for i in range(2)]
            xw = sa("xw", [128, DI], BF16)
            cbm = sa("cbm", [128, 8, 128], BF16)
            R1P = [sa(f"r1p{i}", [128, 512], BF16) for i in range(2)]
            MTP = [sa(f"mtp{i}", [128, 512], BF16) for i in range(4)]
            TMP = [sa(f"tmp{i}", [128, 512], F32) for i in range(2)]
            GN = [sa(f"gn{i}", [128, DI], BF16) for i in range(2)]
            hT = sa("hT", [128, DI], F32)
            hT_bf = sa("hT_bf", [128, DI], BF16)
            ela = sa("ela", [128, 2 * NH], F32)
            elast = sa("elast", [128, NH], F32)
            gss = sa("gss", [128, 8], F32)
            grs = sa("grs", [128, 8], F32)
            junk = sa("junk", [128, 256], BF16)

            def load_a(ti, R, xsrc, *a):
                b = ti % 2
                xin, kx = XIN[b], f"xin{b}"
                P.dma('sp', f"ld{b}", (lambda e: e.dma_start(out=xin[:R, :], in_=xsrc)), writes=[kx])

            def gen_X(ti, R, xsrc, grow, first, last, init, outs):
                b = ti % 2
                xin, kx = XIN[b], f"xin{b}"
                zs, xcb, xdt, xD, B_tm, dtA, dtF = ZS[b], XCB[b], XDT[b], XDD[b], BTM[b], DTA[b], DTF[b]
                if first:
                    if init is None:
                        P.op('pool', lambda e: e.memset(hist[:], 0.0), writes=["hist"])
                    else:
                        P.dma('sp', "ld_hist", (lambda e: e.dma_start(out=hist[:], in_=init[0])), writes=["hist"])
                prenorm(R, xin, kx)
                for zb in range(4):
                    ps, kp = bank()
                    for kc in range(8):
                        P.op('pe', (lambda e, ps=ps, kc=kc, zb=zb: e.matmul(ps[:R, :], lhsT=hnT[:, kc, :R], rhs=Win[:, kc, zb * 512:(zb + 1) * 512],
                                                                           start=(kc == 0), stop=(kc == 7))),
                             reads=["hnT"] + win_keys(zb * 512, zb * 512 + 512), writes=[kp], inc=(kc == 7))
                    P.op('act', (lambda e, ps=ps, zb=zb: e.activation(out=zs[:RE, zb * 512:(zb + 1) * 512], in_=ps[:RE, :], func=AF.Silu)),
                         reads=[kp], writes=[f"zs{b}_{zb}"])
                ps, kp = bank()
                for kc in range(8):
                    P.op('pe', (lambda e, ps=ps, kc=kc: e.matmul(ps[:R, 0:NH], lhsT=hnT[:, kc, :R], rhs=Win[:, kc, 6144:6176],
                                                                 start=(kc == 0), stop=(kc == 7))),
                         reads=["hnT", "win12"], writes=[kp], inc=(kc == 7))
                P.op('dve', (lambda e, ps=ps: e.tensor_tensor(out=dtr[:RE, :], in0=ps[:RE, 0:NH], in1=a3[:RE, 0, :], op=ALU.add)),
                     reads=[kp, "a3k"], writes=["dtr"])
                P.op('act', lambda e: e.activation(out=dtr[:RE, :], in_=dtr[:RE, :], func=AF.Exp), reads=["dtr"], writes=["dtr"])
                P.op('act', lambda e: e.activation(out=dt_[:RE, :], in_=dtr[:RE, :], func=AF.Ln, bias=1.0), reads=["dtr"], writes=["dt"])
                P.op('dve', lambda e: e.tensor_tensor(out=dtF[:RE, :], in0=dt_[:RE, :], in1=A_bc[:RE, :], op=ALU.mult),
                     reads=["dt", "A_bc"], writes=[f"dtF{b}"])
                P.op('dve', lambda e: e.tensor_copy(out=dtA[:RE, :], in_=dtF[:RE, :]), reads=[f"dtF{b}"], writes=[f"dtA{b}"])
                yield
                pend_silu = [None]
                for g in range(8):
                    ps, kp = bank()
                    for j in range(4):
                        ct = 4 * g + j
                        c0 = DI + ct * 128
                        for kc in range(8):
                            P.op('pe', (lambda e, ps=ps, j=j, kc=kc, c0=c0: e.matmul(ps[:, j * 128:j * 128 + R], lhsT=Win[:, kc, c0:c0 + 128],
                                                                                    rhs=hnT[:, kc, :R], start=(kc == 0), stop=(kc == 7))),
                                 reads=["hnT"] + win_keys(c0, c0 + 128), writes=[kp], inc=(j == 3 and kc == 7))
                    raw, kr = RAW[g % 2], f"raw{g % 2}"
                    acc, ka = ACC[g % 2], f"acc{g % 2}"
                    P.op('pool', (lambda e, raw=raw, g=g: e.tensor_copy(out=raw[:, :, 0:3], in_=hist[:, 4 * g:4 * g + 4, :])),
                         reads=["hist"], writes=[kr])
                    P.op('act', (lambda e, ps=ps, raw=raw: e.activation(out=raw[:, :, 3:3 + R],
                                                                        in_=ps[:, :].rearrange("p (a t) -> p a t", a=4)[:, :, :R], func=AF.Copy)),
                         reads=[kp], writes=[kr])
                    P.op('pool', (lambda e, raw=raw, g=g: e.tensor_copy(out=hist[:, 4 * g:4 * g + 4, :], in_=raw[:, :, R:R + 3])),
                         reads=[kr], writes=["hist"])
                    for j in range(4):
                        ct = 4 * g + j
                        P.op('act', (lambda e, raw=raw, acc=acc, j=j, ct=ct: e.activation(
                            out=acc[:, j, :R], in_=raw[:, j, 3:3 + R], func=AF.Identity, scale=cw[:, ct, 3:4], bias=cb[:, ct:ct + 1])),
                            reads=[kr, "cwk", "cbk"], writes=[ka + f"_{j}"])
                        for k in (2, 1, 0):
                            P.op('dve', (lambda e, raw=raw, acc=acc, j=j, ct=ct, k=k: e.scalar_tensor_tensor(
                                out=acc[:, j, :R], in0=raw[:, j, k:k + R], scalar=cw[:, ct, k:k + 1], in1=acc[:, j, :R],
                                op0=ALU.mult, op1=ALU.add)), reads=[kr, "cwk", ka + f"_{j}"], writes=[ka + f"_{j}"])
                    def silu_of(g, acc=acc, ka=ka):
                        if g < 4:
                            P.op('act', (lambda e: e.activation(out=xcx[:, 4 * g:4 * g + 4, :R], in_=acc[:, :, :R], func=AF.Silu)),
                                 reads=[ka + f"_{j}" for j in range(4)], writes=[f"xcx{g}"])
                        else:
                            P.op('act', (lambda e: e.activation(out=xcb[:, 4 * (g - 4):4 * (g - 4) + 4, :R], in_=acc[:, :, :R], func=AF.Silu)),
                                 reads=[ka + f"_{j}" for j in range(4)], writes=[f"xcb{b}_{g - 4}"])
                    if pend_silu[0] is not None:
                        pend_silu[0]()
                    pend_silu[0] = (lambda g=g, f=silu_of: f(g))
                    if g == 3:
                        yield
                pend_silu[0]()
                pend_silu[0] = None
                if last:
                    P.dma('sp', "st_hist", (lambda e: e.dma_start(out=outs[0], in_=hist[:])), reads=["hist"])
                yield
                for g in range(6):
                    ps, kp = bank()
                    for j in range(4):
                        ct = 4 * g + j
                        src = xcx[:, ct, :R] if g < 4 else xcb[:, ct - 16, :R]
                        ksrc = f"xcx{g}" if g < 4 else f"xcb{b}_{g - 4}"
                        P.op('pe', (lambda e, ps=ps, j=j, src=src: e.matmul(ps[:R, j * 128:(j + 1) * 128], lhsT=src, rhs=ident[:, :],
                                                                           start=True, stop=True)),
                             reads=[ksrc, "const"], writes=[kp], inc=(j == 3))
                    if g < 4:
                        P.op('act', (lambda e, ps=ps, g=g: e.activation(out=x_tm[:RE, g * 512:(g + 1) * 512], in_=ps[:RE, :], func=AF.Copy)),
                             reads=[kp], writes=[f"x_tm{g}"])
                        P.op('dve', (lambda e, g=g: e.tensor_tensor(
                            out=xdt[:RE, g * 512:(g + 1) * 512].rearrange("p (h d) -> p h d", h=8),
                            in0=x_tm[:RE, g * 512:(g + 1) * 512].rearrange("p (h d) -> p h d", h=8),
                            in1=dt_[:RE, 8 * g:8 * g + 8].unsqueeze(2).to_broadcast([RE, 8, HD]), op=ALU.mult)),
                            reads=["dt", f"x_tm{g}"], writes=[f"xdt{b}_{g}"])
                        P.op('pool', (lambda e, g=g: e.tensor_tensor(
                            out=xD[:RE, g * 512:(g + 1) * 512].rearrange("p (h d) -> p h d", h=8),
                            in0=x_tm[:RE, g * 512:(g + 1) * 512].rearrange("p (h d) -> p h d", h=8),
                            in1=a3[:RE, 2, 8 * g:8 * g + 8].unsqueeze(2).to_broadcast([RE, 8, HD]), op=ALU.mult)),
                            reads=[f"x_tm{g}", "a3k"], writes=[f"xD{b}_{g}"])
                    else:
                        P.op('act', (lambda e, ps=ps, g=g: e.activation(out=B_tm[:RE, (g - 4) * 512:(g - 3) * 512], in_=ps[:RE, :], func=AF.Copy)),
                             reads=[kp], writes=[f"B_tm{b}_{g - 4}"])

            def gen_Y(ti, R, xsrc, grow, first, last, init, outs):
                b = ti % 2
                zs, xcb, xdt, xD, B_tm, dtA, dtF = ZS[b], XCB[b], XDT[b], XDD[b], BTM[b], DTA[b], DTF[b]
                kB = lambda g: f"xcb{b}_{g // 4}"
                kC = lambda g: f"xcb{b}_{2 + g // 4}"
                if first:
                    if init is None:
                        P.op('pool', lambda e: e.memset(hT[:], 0.0), writes=["hT"])
                        P.op('pool', lambda e: e.memset(hT_bf[:], 0.0), writes=["hT_bf"])
                    else:
                        P.dma('sp', "ld_hT", (lambda e: e.dma_start(out=hT[:], in_=init[1])), writes=["hT"])
                        P.op('act', lambda e: e.activation(out=hT_bf[:], in_=hT[:], func=AF.Copy), reads=["hT"], writes=["hT_bf"])
                ps, kp = bank()
                P.op('pe', (lambda e, ps=ps: e.matmul(ps[:R, 0:NH], lhsT=mle[:R, :R], rhs=dtA[:R, :], start=True, stop=True)),
                     reads=[f"dtA{b}", "const"], writes=[kp], inc=False)
                P.op('pe', (lambda e, ps=ps: e.matmul(ps[:R, NH:2 * NH], lhsT=ugt[:R, :R], rhs=dtA[:R, :], start=True, stop=True)),
                     reads=[f"dtA{b}", "const"], writes=[kp], inc=False)
                P.op('pe', (lambda e, ps=ps: e.matmul(ps[:, 2 * NH:3 * NH], lhsT=ones_bf[:R, :], rhs=dtA[:R, :], start=True, stop=True)),
                     reads=[f"dtA{b}", "ones"], writes=[kp])
                P.op('act', (lambda e, ps=ps: e.activation(out=ela[:RE, :], in_=ps[:RE, 0:2 * NH], func=AF.Exp)), reads=[kp], writes=["ela"])
                P.op('act', (lambda e, ps=ps: e.activation(out=elast[:, :], in_=ps[:, 2 * NH:3 * NH], func=AF.Exp)), reads=[kp], writes=["elast"])
                for bq in range(4):
                    P.op('pool', (lambda e, bq=bq: e.tensor_tensor(
                        out=hT[:, bq * 512:(bq + 1) * 512].rearrange("p (h d) -> p h d", h=8),
                        in0=hT[:, bq * 512:(bq + 1) * 512].rearrange("p (h d) -> p h d", h=8),
                        in1=elast[:, 8 * bq:8 * bq + 8].unsqueeze(2).to_broadcast([128, 8, HD]), op=ALU.mult)),
                        reads=[f"hT{bq}", "hT", "elast"], writes=[f"hT{bq}", "hT"])
                P.op('pool', lambda e: e.tensor_tensor(
                    out=xw[:RE, :].rearrange("p (h d) -> p h d", h=NH), in0=xdt[:RE, :].rearrange("p (h d) -> p h d", h=NH),
                    in1=ela[:RE, NH:2 * NH].unsqueeze(2).to_broadcast([RE, NH, HD]), op=ALU.mult),
                    reads=[f"xdt{b}_{g}" for g in range(4)] + ["ela"], writes=["xw"])
                for half in range(2):
                    ps, kp = bank()
                    for j in range(4):
                        g = half * 4 + j
                        P.op('pe', (lambda e, ps=ps, j=j, g=g: e.matmul(ps[:R, j * 128:j * 128 + R], lhsT=xcb[:, g, :R], rhs=xcb[:, 8 + g, :R],
                                                                       start=True, stop=True)),
                             reads=[kB(g), kC(g)], writes=[kp], inc=(j == 3))
                    P.op('dve', (lambda e, ps=ps, half=half: e.tensor_tensor(
                        out=cbm[:RE, half * 4:half * 4 + 4, :R], in0=ps[:RE, :].rearrange("p (a t) -> p a t", a=4)[:, :, :R],
                        in1=mle[:RE, :R].unsqueeze(1).to_broadcast([RE, 4, R]), op=ALU.mult)),
                        reads=[kp, "const"], writes=[f"cbm{half}"])
                hp = min(NH, 512 // R)
                piece_buf = {}

                def make_piece(pi):
                    r1, kr1 = R1P[pi % 2], f"r1p{pi % 2}"
                    mt, kmt = MTP[pi % 4], f"mtp{pi % 4}"
                    h0 = pi * hp
                    P.op('pool', (lambda e: e.tensor_tensor(
                        out=r1[:RE, :hp * R].rearrange("p (h t) -> p h t", h=hp),
                        in0=dtA[:RE, h0:h0 + hp].unsqueeze(2).to_broadcast([RE, hp, R]),
                        in1=mle[:RE, :R].unsqueeze(1).to_broadcast([RE, hp, R]), op=ALU.mult)),
                        reads=[f"dtA{b}", "const"], writes=[kr1])
                    ps, kp = bank()
                    P.op('pe', (lambda e: e.matmul(ps[:R, :hp * R], lhsT=ugt[:R, :R], rhs=r1[:R, :hp * R], start=True, stop=True)),
                         reads=[kr1, "const"], writes=[kp])
                    P.op('act', (lambda e: e.activation(out=mt[:RE, :hp * R], in_=ps[:RE, :hp * R], func=AF.Exp)), reads=[kp], writes=[kmt])
                    for gi in range(hp // 4):
                        g = h0 // 4 + gi
                        P.op('pool', (lambda e, gi=gi, g=g: e.tensor_tensor(
                            out=mt[:RE, gi * 4 * R:(gi + 1) * 4 * R].rearrange("p (h t) -> p h t", h=4),
                            in0=mt[:RE, gi * 4 * R:(gi + 1) * 4 * R].rearrange("p (h t) -> p h t", h=4),
                            in1=cbm[:RE, g, :R].unsqueeze(1).to_broadcast([RE, 4, R]), op=ALU.mult)),
                            reads=[kmt, f"cbm{g // 4}"], writes=[kmt])
                    piece_buf[pi] = (mt, kmt)

                gnb, kg = GN[b], f"gn{b}"
                def pieces_for(q):
                    for h in range(8 * q, 8 * q + 8):
                        if h // hp not in piece_buf:
                            make_piece(h // hp)
                pieces_for(0)
                pend_tail = [None]
                for q in range(4):
                    if q + 1 < 4:
                        pieces_for(q + 1)
                    psy, kpy = bank()
                    P.op('pe', (lambda e, psy=psy, q=q: e.matmul(psy[:R, :], lhsT=ident[:R, :R], rhs=xD[:R, q * 512:(q + 1) * 512],
                                                                 start=True, stop=False, skip_group_check=True)),
                         reads=[f"xD{b}_{q}", "const"], writes=[kpy], inc=False)
                    for hl in range(8):
                        h = 8 * q + hl
                        pi = h // hp
                        if pi not in piece_buf:
                            make_piece(pi)
                        mt, kmt = piece_buf[pi]
                        ho = (h - pi * hp) * R
                        P.op('pe', (lambda e, psy=psy, hl=hl, h=h, mt=mt, ho=ho: e.matmul(
                            psy[:R, hl * HD:(hl + 1) * HD], lhsT=mt[:R, ho:ho + R], rhs=xdt[:R, h * HD:(h + 1) * HD],
                            start=False, stop=True, skip_group_check=True)),
                            reads=[kmt, f"xdt{b}_{q}"], writes=[kpy], inc=(hl == 7))
                    pss, kps = bank()
                    for gi in range(2):
                        g = 2 * q + gi
                        P.op('pe', (lambda e, pss=pss, gi=gi, g=g: e.matmul(pss[:R, gi * 256:(gi + 1) * 256], lhsT=xcb[:, 8 + g, :R],
                                                                           rhs=hT_bf[:, g * 256:(g + 1) * 256], start=True, stop=True)),
                             reads=[kC(g), "hT_bf"], writes=[kps], inc=(gi == 1))
                    tmp, kt = TMP[q % 2], f"tmp{q % 2}"
                    P.op('dve', (lambda e, pss=pss, tmp=tmp, q=q: e.tensor_tensor(
                        out=tmp[:RE, :].rearrange("p (h d) -> p h d", h=8), in0=pss[:RE, :].rearrange("p (h d) -> p h d", h=8),
                        in1=ela[:RE, 8 * q:8 * q + 8].unsqueeze(2).to_broadcast([RE, 8, HD]), op=ALU.mult)),
                        reads=[kps, "ela"], writes=[kt])
                    P.op('dve', (lambda e, psy=psy, tmp=tmp: e.tensor_tensor(out=tmp[:RE, :], in0=psy[:RE, :], in1=tmp[:RE, :], op=ALU.add)),
                         reads=[kpy, kt], writes=[kt])
                    P.op('pool', (lambda e, tmp=tmp, q=q: e.tensor_tensor(out=tmp[:RE, :], in0=tmp[:RE, :], in1=zs[:RE, q * 512:(q + 1) * 512], op=ALU.mult)),
                         reads=[kt, f"zs{b}_{q}"], writes=[kt])
                    def tail(q=q, tmp=tmp, kt=kt):
                        P.op('pool', (lambda e: e.memset(gss[:RE, 2 * q:2 * q + 2], 0.0)), writes=[f"gss{q}"])
                        for gi in range(2):
                            P.op('act', (lambda e, gi=gi: e.activation(out=junk[:RE, :], in_=tmp[:RE, gi * 256:(gi + 1) * 256], func=AF.Square,
                                                                      accum_out=gss[:RE, 2 * q + gi:2 * q + gi + 1])),
                                 reads=[kt, f"gss{q}"], writes=[f"gss{q}", "junk"])
                        rstd_from(gss[:RE, 2 * q:2 * q + 2], 256, grs[:RE, 2 * q:2 * q + 2], RE, f"gss{q}", f"grs{q}")
                        P.op('pool', (lambda e: e.tensor_tensor(
                            out=gnb[:RE, q * 512:(q + 1) * 512].rearrange("p (g c) -> p g c", g=2), in0=tmp[:RE, :].rearrange("p (g c) -> p g c", g=2),
                            in1=grs[:RE, 2 * q:2 * q + 2].unsqueeze(2).to_broadcast([RE, 2, 256]), op=ALU.mult)),
                            reads=[kt, f"grs{q}"], writes=[kg])
                    if pend_tail[0] is not None:
                        pend_tail[0]()
                    pend_tail[0] = tail
                    if q == 1:
                        yield
                pend_tail[0]()
                P.dma('sp', f"st_gn{b}", (lambda e: e.dma_start(out=gn_scr[grow:grow + R, :], in_=gnb[:R, :])), reads=[kg], writes=["gn_scr"])
                yield
                for bq in range(4):
                    ps, kp = bank()
                    for gi in range(2):
                        g = 2 * bq + gi
                        P.op('pe', (lambda e, ps=ps, gi=gi, g=g: e.matmul(ps[:, gi * 256:(gi + 1) * 256], lhsT=B_tm[:R, g * 128:(g + 1) * 128],
                                                                         rhs=xw[:R, g * 256:(g + 1) * 256], start=True, stop=True)),
                             reads=[f"B_tm{b}_{g // 4}", "xw"], writes=[kp], inc=(gi == 1))
                    P.op('dve', (lambda e, ps=ps, bq=bq: e.tensor_tensor(out=hT[:, bq * 512:(bq + 1) * 512], in0=ps[:, :],
                                                                        in1=hT[:, bq * 512:(bq + 1) * 512], op=ALU.add)),
                         reads=[kp, f"hT{bq}"], writes=[f"hT{bq}"])
                    P.op('act', (lambda e, bq=bq: e.activation(out=hT_bf[:, bq * 512:(bq + 1) * 512], in_=hT[:, bq * 512:(bq + 1) * 512], func=AF.Copy)),
                         reads=[f"hT{bq}"], writes=["hT_bf"])
                if last:
                    P.dma('sp', "st_hT", (lambda e: e.dma_start(out=outs[1], in_=hT[:])), reads=["hT"] + [f"hT{i}" for i in range(4)], writes=[])

            tiles = []
            npt = SEQ // 128
            for s in range(NSEQ_P):
                for i in range(npt):
                    r0 = s * SEQ + i * 128
                    tiles.append((128, xp[r0:r0 + 128, :], r0, i == 0, i == npt - 1, None, (ncp[s], nsp[s])))
            for j in range(NSEQ_S):
                r0 = j * LS
                tiles.append((LS, xs[r0:r0 + LS, :], NPR + r0, True, not dbg.get('nolast'), None if dbg.get('noinit') else (cconv[j], sssm[j]), (ncs[j], nss[j])))
            if dbg.get('skipP'):
                tiles = tiles[NSEQ_P * npt:]
            tiles = tiles[:dbg.get('nA', len(tiles))]


            if tiles:
                load_a(0, *tiles[0])
                if len(tiles) > 1:
                    load_a(1, *tiles[1])
                drain(gen_X(0, *tiles[0]))
            for ti, t in enumerate(tiles):
                if ti + 2 < len(tiles):
                    pass
                if ti + 1 < len(tiles):
                    gx = gen_X(ti + 1, *tiles[ti + 1])
                    interleave(gen_Y(ti, *t), gx)
                    if ti + 2 < len(tiles):
                        load_a(ti + 2, *tiles[ti + 2])
                else:
                    drain(gen_Y(ti, *t))

        P.barrier(lambda e: e.memset(bar_t[:], 0.0))
        esBC = ExitStack()
        es.enter_context(esBC)
        Wout = esBC.enter_context(nc.sbuf_tensor("sb_Wout", [128, 16, D], BF16))
        WO[0] = Wout
        load_wout(w_a_out)
        for ct in range(16):
            P.op('dve', (lambda e, ct=ct: e.tensor_scalar(out=Wout[:, ct, :], in0=Wout[:, ct, :], scalar1=anw[:, ct:ct + 1], scalar2=None,
                                                          op0=ALU.mult)), reads=[f"wout{ct // 4}", "anwk"], writes=[f"wout{ct // 4}"])
        load_weights(w_b_in, B_PROJ, None, "b")
        P.dma('sp', "c_nw", lambda e: e.dma_start(out=nw_bc[:], in_=nw_d[1].partition_broadcast(128)), writes=["nw_bc"])
        esB = ExitStack()
        with esB:
            def sbb(name, shape, dt):
                return esB.enter_context(nc.sbuf_tensor("sbb_" + name, shape, dt))
            NB = 4
            GIN = [sbb(f"gin{i}", [128, DI], BF16) for i in range(NB)]
            XB = [sbb(f"xb{i}", [128, D], F32) for i in range(NB)]
            YT = [sbb(f"yT{i}", [128, 16, 128], BF16) for i in range(2)]

            def load_b(ti, R, xsrc, grow):
                b = ti % NB
                xin, kx = XB[b], f"xb{b}"
                gin, kgi = GIN[b], f"gin{b}"
                P.dma('sp', f"ldx{b}", (lambda e: e.dma_start(out=xin[:R, :], in_=xsrc)), writes=[kx])
                P.dma('sp', f"ldg{b}", (lambda e: e.dma_start(out=gin[:R, :], in_=gn_scr[grow:grow + R, :])), reads=["gn_scr"], writes=[kgi])

            def tile_b(ti, R, xsrc, grow):
                b = ti % NB
                xin, kx = XB[b], f"xb{b}"
                gin, kgi = GIN[b], f"gin{b}"
                yT = YT[ti % 2]
                ky = f"yT{ti % 2}_"
                for g in range(4):
                    ps, kp = bank()
                    for j in range(4):
                        ct = 4 * g + j
                        P.op('pe', (lambda e, ps=ps, j=j, ct=ct: e.matmul(ps[:, j * 128:j * 128 + R], lhsT=gin[:R, ct * 128:(ct + 1) * 128],
                                                                         rhs=ident[:R, :R], start=True, stop=True)),
                             reads=[kgi, "const"], writes=[kp], inc=(j == 3))
                    P.op('act', (lambda e, ps=ps, g=g: e.activation(out=yT[:, 4 * g:4 * g + 4, :R],
                                                                    in_=ps[:, :].rearrange("p (a t) -> p a t", a=4)[:, :, :R], func=AF.Copy)),
                         reads=[kp], writes=[ky + f"{g}"])
                for half in range(2):
                    ps, kp = bank()
                    for ct in range(16):
                        P.op('pe', (lambda e, ps=ps, ct=ct, half=half: e.matmul(ps[:R, :], lhsT=yT[:, ct, :R], rhs=Wout[:, ct, half * 512:(half + 1) * 512],
                                                                               start=(ct == 0), stop=(ct == 15))),
                             reads=[ky + f"{ct // 4}", f"wout{ct // 4}"], writes=[kp], inc=(ct == 15))
                    P.op('dve', (lambda e, ps=ps, half=half: e.tensor_tensor(out=xin[:RE, half * 512:(half + 1) * 512], in0=ps[:RE, :],
                                                                            in1=xin[:RE, half * 512:(half + 1) * 512], op=ALU.add)),
                         reads=[kp, kx], writes=[kx])
                P.dma('sp', f"st_h{b}", (lambda e: e.dma_start(out=h1_scr[grow:grow + R, :], in_=xin[:R, :])), reads=[kx], writes=["h1_scr"])

            tiles = []
            for s in range(NSEQ_P):
                for i in range(SEQ // 128):
                    r0 = s * SEQ + i * 128
                    tiles.append((128, xp[r0:r0 + 128, :], r0))
            for j in range(NSEQ_S):
                r0 = j * LS
                tiles.append((LS, xs[r0:r0 + LS, :], NPR + r0))
            tiles = tiles[:dbg.get('nB', len(tiles))]
            if tiles:
                load_b(0, *tiles[0])
            for ti, t in enumerate(tiles):
                if ti + 1 < len(tiles):
                    load_b(ti + 1, *tiles[ti + 1])
                tile_b(ti, *t)

        P.barrier(lambda e: e.memset(bar_t[:], 0.0))
        load_wout(w_b_out)
        esC = ExitStack()
        with esC:
            def sc(name, shape, dt):
                return esC.enter_context(nc.sbuf_tensor("sc_" + name, shape, dt))
            fnw_bc = sc("fnw_bc", [128, D], F32)
            lnw_bc = sc("lnw_bc", [128, DI], F32)
            lnb_bc = sc("lnb_bc", [128, DI], F32)
            wmT = sc("wmT", [128, 8, 128], BF16)
            bspc = sc("bspc", [128, 8], F32)
            UT = [sc(f"uT{i}", [128, DI], BF16) for i in range(2)]
            szT = sc("szT", [128, DI], BF16)
            v32 = sc("v32", [128, DI], F32)
            VBF = [sc(f"vbf{i}", [128, DI], BF16) for i in range(2)]
            yT2 = sc("yT2", [128, 16, 128], BF16)
            XC = [XIN[0], XIN[1], sc("xin2", [128, D], F32)]
            xn2 = sc("xn2", [128, D], BF16)
            st = sc("st", [128, 4], F32)
            mv = sc("mv", [128, 4], F32)

            P.dma('sp', "c_fnw", lambda e: e.dma_start(out=fnw_bc[:], in_=fnw_d.partition_broadcast(128)), writes=["fnw"])
            P.dma('sp', "c_lnw", lambda e: e.dma_start(out=lnw_bc[:], in_=lnw_d.partition_broadcast(128)), writes=["lnw"])
            P.dma('sp', "c_lnb", lambda e: e.dma_start(out=lnb_bc[:], in_=lnb_d.partition_broadcast(128)), writes=["lnb"])
            P.dma('pool', "c_wsp", lambda e: e.dma_start(out=wmT[:], in_=wsp_d), writes=["wmT"])
            P.dma('sp', "c_bsp", lambda e: e.dma_start(out=bspc[:], in_=bsp_d), writes=["bsp"])
            P.op('pool', lambda e: e.memset(wmT[64:128, :, 0:64], 0.0), reads=["wmT"], writes=["wmT"])

            def load_c(ti, R, grow, *a):
                b = ti % 2
                xin, kx = XC[ti % 3], f"xin{ti % 3}"
                P.dma('sp', f"ld{b}", (lambda e: e.dma_start(out=xin[:R, :], in_=h1_scr[grow:grow + R, :])), reads=["h1_scr"], writes=[kx])

            def gen_CX(ti, R, grow, ydst, vdst):
                b = ti % 2
                xin, kx = XC[ti % 3], f"xin{ti % 3}"
                uT, vbf = UT[b], VBF[b]
                prenorm(R, xin, kx)
                yield
                for (c_base, func, kd) in ((0, AF.Gelu, f"uT{b}_"), (2 * DI, AF.Silu, "szT")):
                    for cb_ in range(4):
                        ps, kp = bank()
                        c0 = c_base + cb_ * 512
                        for kc in range(8):
                            P.op('pe', (lambda e, ps=ps, kc=kc, c0=c0: e.matmul(ps[:R, :], lhsT=hnT[:, kc, :R], rhs=Win[:, kc, c0:c0 + 512],
                                                                               start=(kc == 0), stop=(kc == 7))),
                                 reads=["hnT"] + win_keys(c0, c0 + 512), writes=[kp], inc=(kc == 7))
                        dst = uT if kd != "szT" else szT
                        P.op('act', (lambda e, ps=ps, cb_=cb_, func=func, dst=dst: e.activation(
                            out=dst[:RE, cb_ * 512:(cb_ + 1) * 512], in_=ps[:RE, :], func=func)),
                            reads=[kp], writes=[f"{kd}{cb_}"])
                        if kd == "szT":
                            P.op('pool', (lambda e, cb_=cb_: e.tensor_tensor(out=uT[:RE, cb_ * 512:(cb_ + 1) * 512], in0=uT[:RE, cb_ * 512:(cb_ + 1) * 512],
                                                                           in1=szT[:RE, cb_ * 512:(cb_ + 1) * 512], op=ALU.mult)),
                                 reads=[f"uT{b}_{cb_}", f"szT{cb_}"], writes=[f"uT{b}_{cb_}"])
                        if cb_ % 2 == 1:
                            yield
                P.op('pool', lambda e: e.memset(st[:RE, 0:4], 0.0), writes=["st"])
                for vb in range(4):
                    ps, kp = bank()
                    c0 = DI + vb * 512
                    for kc in range(8):
                        P.op('pe', (lambda e, ps=ps, kc=kc, c0=c0: e.matmul(ps[:R, :], lhsT=hnT[:, kc, :R], rhs=Win[:, kc, c0:c0 + 512],
                                                                           start=(kc == 0), stop=(kc == 7))),
                             reads=["hnT"] + win_keys(c0, c0 + 512), writes=[kp], inc=(kc == 7))
                    P.op('act', (lambda e, ps=ps, vb=vb: e.activation(out=v32[:RE, vb * 512:(vb + 1) * 512], in_=ps[:RE, :], func=AF.Gelu)),
                         reads=[kp], writes=[f"v{vb}"])
                    if vb == 1:
                        yield
                yield
                vk = [f"v{i}" for i in range(4)]
                kvb = f"vbf{b}"
                P.op('act', lambda e: e.activation(out=vbf[:RE, :], in_=v32[:RE, :], func=AF.Copy, accum_out=st[:RE, 0:1]), reads=vk + ["st"], writes=[kvb, "st"])
                P.op('act', lambda e: e.activation(out=vbf[:RE, :], in_=v32[:RE, :], func=AF.Square, accum_out=st[:RE, 1:2]), reads=vk + ["st"], writes=[kvb, "st"])
                P.op('dve', lambda e: e.tensor_scalar(out=mv[:RE, 0:2], in0=st[:RE, 0:2], scalar1=1.0 / DI, scalar2=None, op0=ALU.mult),
                     reads=["st"], writes=["mv"])
                P.op('dve', lambda e: e.tensor_tensor(out=mv[:RE, 2:3], in0=mv[:RE, 0:1], in1=mv[:RE, 0:1], op=ALU.mult), reads=["mv"], writes=["mv"])
                P.op('dve', lambda e: e.tensor_tensor(out=mv[:RE, 2:3], in0=mv[:RE, 1:2], in1=mv[:RE, 2:3], op=ALU.subtract), reads=["mv"], writes=["mv"])
                P.op('dve', lambda e: e.tensor_scalar(out=mv[:RE, 2:3], in0=mv[:RE, 2:3], scalar1=EPS, scalar2=None, op0=ALU.add), reads=["mv"], writes=["mv"])
                P.op('pool', lambda e: e.tensor_tensor(out=mv[:RE, 3:4], in0=mv[:RE, 2:3], in1=mhalf[:RE, 0:1], op=ALU.pow), reads=["mv", "mhalf"], writes=["mv"])
                for vb in range(4):
                    sl = slice(vb * 512, (vb + 1) * 512)
                    kv = [f"v{vb}"]
                    P.op('dve', (lambda e, sl=sl: e.tensor_scalar(out=v32[:RE, sl], in0=v32[:RE, sl], scalar1=mv[:RE, 0:1], scalar2=mv[:RE, 3:4],
                                                                  op0=ALU.subtract, op1=ALU.mult)), reads=kv + ["mv"], writes=kv)
                    if vb % 2 == 0:
                        P.op('pool', (lambda e, sl=sl: e.tensor_tensor(out=v32[:RE, sl], in0=v32[:RE, sl], in1=lnw_bc[:RE, sl], op=ALU.mult)),
                             reads=kv + ["lnw"], writes=kv)
                    else:
                        P.op('dve', (lambda e, sl=sl: e.tensor_tensor(out=v32[:RE, sl], in0=v32[:RE, sl], in1=lnw_bc[:RE, sl], op=ALU.mult)),
                             reads=kv + ["lnw"], writes=kv)
                    P.op('dve', (lambda e, sl=sl: e.tensor_tensor(out=v32[:RE, sl], in0=v32[:RE, sl], in1=lnb_bc[:RE, sl], op=ALU.add)),
                         reads=kv + ["lnb"], writes=kv)
                    P.op('act', (lambda e, sl=sl: e.activation(out=vbf[:RE, sl], in_=v32[:RE, sl], func=AF.Copy)), reads=kv, writes=[kvb + f"_{vb}"])
                if vdst is not None:
                    P.dma('sp', "st_v", (lambda e: e.dma_start(out=vdst, in_=v32[:R, :])), reads=vk)

            def gen_CY(ti, R, grow, ydst, vdst):
                b = ti % 2
                xin, kx = XC[ti % 3], f"xin{ti % 3}"
                uT, vbf = UT[b], VBF[b]
                kvb = f"vbf{b}"
                for g2 in range(4):
                    ps, kp = bank()
                    for j in range(2):
                        g = 2 * g2 + j
                        P.op('pe', (lambda e, ps=ps, j=j, g=g: e.matmul(ps[:R, j * 256:(j + 1) * 256], lhsT=wmT[:R, g, :R],
                                                                       rhs=vbf[:R, g * 256:(g + 1) * 256], start=True, stop=True)),
                             reads=[kvb, kvb + f"_{g // 2}", "wmT"], writes=[kp], inc=(j == 1))
                    for j in range(2):
                        g = 2 * g2 + j
                        P.op('dve', (lambda e, ps=ps, j=j, g=g: e.scalar_tensor_tensor(
                            out=uT[:RE, g * 256:(g + 1) * 256], in0=ps[:RE, j * 256:(j + 1) * 256], scalar=bspc[:RE, g:g + 1],
                            in1=uT[:RE, g * 256:(g + 1) * 256], op0=ALU.add, op1=ALU.mult)),
                            reads=[kp, "bsp", f"uT{b}_{g2}"], writes=[f"uT{b}_{g2}"])
                    if g2 == 1:
                        yield
                yield
                for g in range(4):
                    ps, kp = bank()
                    for j in range(4):
                        ct = 4 * g + j
                        P.op('pe', (lambda e, ps=ps, j=j, ct=ct: e.matmul(ps[:, j * 128:j * 128 + R], lhsT=uT[:R, ct * 128:(ct + 1) * 128],
                                                                         rhs=ident[:R, :R], start=True, stop=True)),
                             reads=[f"uT{b}_{g}", "const"], writes=[kp], inc=(j == 3))
                    P.op('act', (lambda e, ps=ps, g=g: e.activation(out=yT2[:, 4 * g:4 * g + 4, :R],
                                                                    in_=ps[:, :].rearrange("p (a t) -> p a t", a=4)[:, :, :R], func=AF.Copy)),
                         reads=[kp], writes=[f"yT2{g}"])
                yield
                for half in range(2):
                    ps, kp = bank()
                    for ct in range(16):
                        P.op('pe', (lambda e, ps=ps, ct=ct, half=half: e.matmul(ps[:R, :], lhsT=yT2[:, ct, :R], rhs=Wout[:, ct, half * 512:(half + 1) * 512],
                                                                               start=(ct == 0), stop=(ct == 15))),
                             reads=[f"yT2{ct // 4}", f"wout{ct // 4}"], writes=[kp], inc=(ct == 15))
                    P.op('dve', (lambda e, ps=ps, half=half: e.tensor_tensor(out=xin[:RE, half * 512:(half + 1) * 512], in0=ps[:RE, :],
                                                                            in1=xin[:RE, half * 512:(half + 1) * 512], op=ALU.add)),
                         reads=[kp, kx], writes=[kx])
                    yield
                P.op('pool', lambda e: e.memset(ss[:RE, 1:2], 0.0), writes=["ss1"])
                P.op('act', lambda e: e.activation(out=xn2[:RE, :], in_=xin[:RE, :], func=AF.Square, accum_out=ss[:RE, 1:2]),
                     reads=[kx, "ss1"], writes=["xn2", "ss1"])
                rstd_from(ss[:RE, 1:2], D, rs[:RE, 1:2], RE, "ss1", "rs1")
                P.op('dve', lambda e: e.scalar_tensor_tensor(out=xin[:RE, :], in0=xin[:RE, :], scalar=rs[:RE, 1:2], in1=fnw_bc[:RE, :],
                                                              op0=ALU.mult, op1=ALU.mult), reads=[kx, "rs1", "fnw"], writes=[kx])
                P.dma('sp', f"st_y{ti % 3}", (lambda e: e.dma_start(out=ydst, in_=xin[:R, :])), reads=[kx])

            tiles = []
            for s in range(NSEQ_P):
                for i in range(SEQ // 128):
                    r0 = s * SEQ + i * 128
                    tiles.append((128, r0, yp[r0:r0 + 128, :], None))
            for j in range(NSEQ_S):
                r0 = j * LS
                tiles.append((LS, NPR + r0, ys[r0:r0 + LS, :], nvs[r0:r0 + LS, :]))
            tiles = tiles[:dbg.get('nC', len(tiles))]
            if tiles:
                load_c(0, *tiles[0])
                drain(gen_CX(0, *tiles[0]))
            for ti, t in enumerate(tiles):
                if ti + 1 < len(tiles):
                    load_c(ti + 1, *tiles[ti + 1])
                    interleave(gen_CY(ti, *t), gen_CX(ti + 1, *tiles[ti + 1]))
                else:
                    drain(gen_CY(ti, *t))

            P.emit()
    return nc


def _consts():
    k = np.arange(128)
    ident = np.eye(128, dtype=np.float32)
    mle = (k[:, None] <= k[None, :]).astype(np.float32)
    ugt = (k[:, None] > k[None, :]).astype(np.float32)
    return ident, mle, ugt


def _klayout(w):
    K, C = w.shape
    return np.ascontiguousarray(w.reshape(K // 128, 128, C).transpose(1, 0, 2))


def make_in_maps(inputs, n_cores, SEQ):
    f = lambda a: np.ascontiguousarray(np.asarray(a, dtype=np.float32))
    ident, mle, ugt = _consts()
    shared = {
        "w_a_in": _klayout(f(inputs["a_w_in"][0])),
        "w_a_out": _klayout(f(inputs["a_w_out"][0])),
        "w_b_in": _klayout(f(inputs["b_w_in"][0])),
        "w_b_out": _klayout(f(inputs["b_w_out"][0])),
        "cw": np.ascontiguousarray(f(inputs["a_conv_w"][0]).reshape(4, 32, 128).transpose(2, 1, 0)),
        "cb": np.ascontiguousarray(f(inputs["a_conv_b"][0]).reshape(32, 128).T),
        "anw": np.ascontiguousarray(f(inputs["a_norm_w"][0]).reshape(16, 128).T),
        "nw": f(inputs["norm_w"]),
        "fnw": f(inputs["final_norm_w"]),
        "a3": np.ascontiguousarray(np.stack([f(inputs["a_dt_bias"][0]), f(inputs["a_log"][0]), f(inputs["a_d"][0])])),
        "lnw": f(inputs["b_ln_w"][0]),
        "lnb": f(inputs["b_ln_b"][0]),
        "wsp": np.ascontiguousarray(f(inputs["b_w_sp"][0]).transpose(2, 0, 1)),
        "bsp": np.ascontiguousarray(f(inputs["b_b_sp"][0]).T),
        "ident": ident, "mle": mle, "ugt": ugt,
    }
    xp = f(inputs["x_prompt"])
    xs = f(inputs["x_sample"])
    cc = f(inputs["cache_conv"][0])
    sm = f(inputs["state_ssm"][0])
    maps = []
    for c in range(n_cores):
        m = dict(shared)
        m["xp"] = np.ascontiguousarray(xp[NSEQ_P * c:NSEQ_P * (c + 1)].reshape(NSEQ_P * SEQ, D))
        m["xs"] = np.ascontiguousarray(xs[NSEQ_S * c:NSEQ_S * (c + 1)].reshape(NSEQ_S * LS, D))
        ccs = cc[NSEQ_S * c:NSEQ_S * (c + 1)]
        m["cconv"] = np.ascontiguousarray(ccs.reshape(NSEQ_S, 3, 32, 128).transpose(0, 3, 2, 1))
        sms = sm[NSEQ_S * c:NSEQ_S * (c + 1)]
        m["sssm"] = np.ascontiguousarray(sms.reshape(NSEQ_S, DI, NS).transpose(0, 2, 1))
        maps.append(m)
    return maps


def assemble(results, n_cores, SEQ):
    yp = np.concatenate([r["yp"].reshape(NSEQ_P, SEQ, D) for r in results], 0)
    ys = np.concatenate([r["ys"].reshape(NSEQ_S, LS, D) for r in results], 0)

    def conv_back(a):
        n = a.shape[0]
        return np.ascontiguousarray(a.transpose(0, 3, 2, 1).reshape(n, 3, 4096))

    def ssm_back(a):
        n = a.shape[0]
        return np.ascontiguousarray(a.transpose(0, 2, 1).reshape(n, NH, HD, NS))
    ncp = np.concatenate([conv_back(r["ncp"]) for r in results], 0)[None]
    nsp = np.concatenate([ssm_back(r["nsp"]) for r in results], 0)[None]
    ncs = np.concatenate([conv_back(r["ncs"]) for r in results], 0)[None]
    nss = np.concatenate([ssm_back(r["nss"]) for r in results], 0)[None]
    nvs = np.concatenate([r["nvs"].reshape(NSEQ_S, LS, DI) for r in results], 0)[None]
    return tuple(np.asarray(a, dtype=np.float32) for a in (yp, ys, ncp, nsp, ncs, nss, nvs))


def kernel(**inputs):
    n_cores = 8
    SEQ = inputs["x_prompt"].shape[1]
    nc = build(SEQ)
    maps = make_in_maps(inputs, n_cores, SEQ)
    res = run_bass_kernel_spmd(nc, maps, core_ids=list(range(n_cores)))
    return assemble(res.results, n_cores, SEQ)
```
